# Optimizing a Trainium2 kernel written in Bass

```python
import jax, jax.numpy as jnp
from jax import lax
import numpy as np

D_MODEL = 1024
BATCH = 16
SEQ = 4096
DEPTH = 4
DEC_BATCH = 8
DEC_SEQ = 8192
PAST_LEN = 128

N_MIXERS = 2
HEAD_DIM = 64
N_Q_HEADS = D_MODEL // HEAD_DIM
N_KV_HEADS = N_Q_HEADS // 4
Q_PER_KV = N_Q_HEADS // N_KV_HEADS
QKV_DIM = (N_Q_HEADS + 2 * N_KV_HEADS) * HEAD_DIM
ROPE_THETA = 10000.0
GRID_W = 64
Q_BLOCK = 128
POOL_WINDOWS = (2, 4, 8, 16)
N_POOL_GROUPS = len(POOL_WINDOWS)
POOL_GROUP_DIM = D_MODEL // N_POOL_GROUPS
D_FF = ((8 * D_MODEL // 3 + 127) // 128) * 128
N_ATTN_LAYERS = (DEPTH + 1) // 2
N_POOL_LAYERS = DEPTH // 2
EPS = 1e-6

kernel_name = "hybrid_gqa_axialrope_multiscale_pool_macaron_encoder"


def rmsnorm(x, g):
    xf = x.astype(jnp.float32)
    r = lax.rsqrt(jnp.mean(xf * xf, axis=-1, keepdims=True) + EPS)
    return (xf * r).astype(x.dtype) * g.astype(x.dtype)


def axial_rope_tables(S):
    rows = S // GRID_W
    row = jnp.repeat(jnp.arange(rows, dtype=jnp.float32), GRID_W)
    col = jnp.tile(jnp.arange(GRID_W, dtype=jnp.float32), rows)
    n_freq = HEAD_DIM // 4
    freqs = ROPE_THETA ** (-jnp.arange(n_freq, dtype=jnp.float32) / n_freq)
    ang = jnp.concatenate([row[:, None] * freqs, col[:, None] * freqs], axis=-1)
    return jnp.cos(ang), jnp.sin(ang)


def apply_rope(x, cos, sin):
    xp = x.astype(jnp.float32).reshape(*x.shape[:-1], HEAD_DIM // 2, 2)
    x1, x2 = xp[..., 0], xp[..., 1]
    c = cos[None, :, None, :]
    s = sin[None, :, None, :]
    out = jnp.stack([x1 * c - x2 * s, x1 * s + x2 * c], axis=-1)
    return out.reshape(x.shape).astype(x.dtype)


def swiglu(h, w_gate, w_up, w_down):
    return (jax.nn.silu(h @ w_gate) * (h @ w_up)) @ w_down


def attention_mixer(h, w_qkv, q_gain, k_gain, w_o, cos, sin):
    B, S, _ = h.shape
    qkv = h @ w_qkv
    nq = N_Q_HEADS * HEAD_DIM
    nk = N_KV_HEADS * HEAD_DIM
    q = qkv[..., :nq].reshape(B, S, N_Q_HEADS, HEAD_DIM)
    k = qkv[..., nq:nq + nk].reshape(B, S, N_KV_HEADS, HEAD_DIM)
    v = qkv[..., nq + nk:].reshape(B, S, N_KV_HEADS, HEAD_DIM)
    q = apply_rope(rmsnorm(q, q_gain), cos, sin) * (HEAD_DIM ** -0.5)
    k = apply_rope(rmsnorm(k, k_gain), cos, sin)
    q = q.reshape(B, S // Q_BLOCK, Q_BLOCK, N_KV_HEADS, Q_PER_KV, HEAD_DIM)
    qb = jnp.moveaxis(q, 1, 0)

    def block(qblk):
        s = jnp.einsum('bqgrd,bkgd->bgrqk', qblk, k).astype(jnp.float32)
        p = jax.nn.softmax(s, axis=-1).astype(v.dtype)
        return jnp.einsum('bgrqk,bkgd->bqgrd', p, v)

    o = lax.map(block, qb)
    o = jnp.moveaxis(o, 0, 1).reshape(B, S, N_Q_HEADS * HEAD_DIM)
    return o @ w_o


def pool_mixer(h, w_in, w_group, w_out, scale):
    B, S, _ = h.shape
    u = h @ w_in
    uf = u.astype(jnp.float32)
    cs = jnp.concatenate([jnp.zeros((B, 1, D_MODEL), jnp.float32), jnp.cumsum(uf, axis=1)], axis=1)
    t = jnp.arange(S)
    outs = []
    for g, w in enumerate(POOL_WINDOWS):
        lo = jnp.clip(t - w // 2, 0, S)
        hi = jnp.clip(t + w // 2, 0, S)
        sl = slice(g * POOL_GROUP_DIM, (g + 1) * POOL_GROUP_DIM)
        cg = cs[..., sl]
        cnt = (hi - lo).astype(jnp.float32)[None, :, None]
        mean = (jnp.take(cg, hi, axis=1) - jnp.take(cg, lo, axis=1)) / cnt
        outs.append(mean - uf[..., sl])
    d = jnp.stack(outs, axis=2).astype(h.dtype)
    z = jnp.einsum('bsgc,gcd->bsgd', d, w_group).reshape(B, S, D_MODEL)
    return (z @ w_out) * scale


def run_trunk(x, norm_gains, ffn_w_gate, ffn_w_up, ffn_w_down, attn_w_qkv, attn_q_gain,
              attn_k_gain, attn_w_o, pool_w_in, pool_w_group, pool_w_out, pool_scale):
    cos, sin = axial_rope_tables(x.shape[1])
    for i in range(DEPTH):
        x = x + 0.5 * swiglu(rmsnorm(x, norm_gains[i, 0]), ffn_w_gate[i, 0], ffn_w_up[i, 0], ffn_w_down[i, 0])
        h = rmsnorm(x, norm_gains[i, 1])
        j = i // N_MIXERS
        if i % N_MIXERS == 0:
            x = x + attention_mixer(h, attn_w_qkv[j], attn_q_gain[j], attn_k_gain[j], attn_w_o[j], cos, sin)
        else:
            x = x + pool_mixer(h, pool_w_in[j], pool_w_group[j], pool_w_out[j], pool_scale[j])
        x = x + 0.5 * swiglu(rmsnorm(x, norm_gains[i, 2]), ffn_w_gate[i, 1], ffn_w_up[i, 1], ffn_w_down[i, 1])
    return x


def setup_inputs(seed: int = 0) -> dict:
    key = jax.random.key(seed)
    ks = jax.random.split(key, 16)
    f32 = jnp.float32
    nrm = lambda k, shape, s: jax.random.normal(k, shape, f32) * s
    return {
        "x_prompt": nrm(ks[0], (BATCH, SEQ, D_MODEL), 1.0),
        "x_sample": nrm(ks[1], (DEC_BATCH, DEC_SEQ, D_MODEL), 1.0),
        "norm_gains": 1.0 + nrm(ks[2], (DEPTH, 3, D_MODEL), 0.05),
        "ffn_w_gate": nrm(ks[3], (DEPTH, 2, D_MODEL, D_FF), D_MODEL ** -0.5),
        "ffn_w_up": nrm(ks[4], (DEPTH, 2, D_MODEL, D_FF), D_MODEL ** -0.5),
        "ffn_w_down": nrm(ks[5], (DEPTH, 2, D_FF, D_MODEL), D_FF ** -0.5),
        "attn_w_qkv": nrm(ks[6], (N_ATTN_LAYERS, D_MODEL, QKV_DIM), D_MODEL ** -0.5),
        "attn_q_gain": 1.0 + nrm(ks[7], (N_ATTN_LAYERS, HEAD_DIM), 0.05),
        "attn_k_gain": 1.0 + nrm(ks[8], (N_ATTN_LAYERS, HEAD_DIM), 0.05),
        "attn_w_o": nrm(ks[9], (N_ATTN_LAYERS, N_Q_HEADS * HEAD_DIM, D_MODEL), D_MODEL ** -0.5),
        "pool_w_in": nrm(ks[10], (N_POOL_LAYERS, D_MODEL, D_MODEL), D_MODEL ** -0.5),
        "pool_w_group": nrm(ks[11], (N_POOL_LAYERS, N_POOL_GROUPS, POOL_GROUP_DIM, POOL_GROUP_DIM), POOL_GROUP_DIM ** -0.5),
        "pool_w_out": nrm(ks[12], (N_POOL_LAYERS, D_MODEL, D_MODEL), D_MODEL ** -0.5),
        "pool_scale": 0.5 + nrm(ks[13], (N_POOL_LAYERS, D_MODEL), 0.05),
    }


def reference(x_prompt, x_sample, norm_gains, ffn_w_gate, ffn_w_up, ffn_w_down, attn_w_qkv,
              attn_q_gain, attn_k_gain, attn_w_o, pool_w_in, pool_w_group, pool_w_out, pool_scale):
    y_prompt = run_trunk(x_prompt, norm_gains, ffn_w_gate, ffn_w_up, ffn_w_down, attn_w_qkv, attn_q_gain,
                         attn_k_gain, attn_w_o, pool_w_in, pool_w_group, pool_w_out, pool_scale)
    y_sample = run_trunk(x_sample, norm_gains, ffn_w_gate, ffn_w_up, ffn_w_down, attn_w_qkv, attn_q_gain,
                         attn_k_gain, attn_w_o, pool_w_in, pool_w_group, pool_w_out, pool_scale)
    return (y_prompt, y_sample)
```

```python
import numpy as np
import concourse.bass as bass
import concourse.mybir as mybir
from concourse.bass_utils import run_bass_kernel_spmd

F32 = mybir.dt.float32
BF16 = mybir.dt.bfloat16
ALU = mybir.AluOpType
AF = mybir.ActivationFunctionType

D = 1024
DFF = 2816
NFC = 22
NKC = 8
T = 1024
NSUB = T // 512
EPS = 1e-6
DEPTH = 4
POOLW = (2, 4, 8, 16)
SMAX = 8192
SEQS_FULL = (4096, 4096, 8192)
EPOCH = 16000
NDMASEM = 24
CAST_CHUNK = 1 << 21


def weight_layout():
    off = {}
    cur = 0

    def add(key, n):
        nonlocal cur
        off[key] = cur
        cur += n

    def ffn(i, j):
        for fc in range(NFC):
            add(("gu", i, j, fc), 128 * 2048)
        for m in range(8):
            add(("dn", i, j, m), 128 * 2816)

    def attn_qkv(l):
        for m in range(10):
            add(("qk", l, m), 128 * 1024)
        add(("v", l), 128 * 2048)

    def attn_o(l):
        for m in range(8):
            add(("wo", l, m), 128 * 1024)

    def pool_in(l):
        for m in range(8):
            add(("win", l, m), 128 * 1024)

    def pool_rest(l):
        add(("wg", l), 128 * 2048)
        for m in range(8):
            add(("wout", l, m), 128 * 1024)

    ffn(0, 0); attn_qkv(0); attn_o(0); ffn(0, 1)
    ffn(1, 0); pool_in(0); pool_rest(0); ffn(1, 1)
    ffn(2, 0); attn_qkv(1); attn_o(1); ffn(2, 1)
    ffn(3, 0); pool_in(1); pool_rest(1); ffn(3, 1)
    return off, cur


def pack_weights(inp):
    off, total = weight_layout()
    wall = np.empty(total, np.float32)

    def put(key, arr):
        a = np.ascontiguousarray(arr, dtype=np.float32).reshape(-1)
        wall[off[key]:off[key] + a.size] = a

    for i in range(DEPTH):
        for j in range(2):
            wg = inp["ffn_w_gate"][i, j].reshape(8, 128, NFC, 128)
            wu = inp["ffn_w_up"][i, j].reshape(8, 128, NFC, 128)
            gu = np.stack([wg, wu], 0).transpose(3, 2, 0, 1, 4)
            for fc in range(NFC):
                put(("gu", i, j, fc), gu[fc])
            wd = inp["ffn_w_down"][i, j].reshape(NFC, 128, 8, 128).transpose(2, 1, 0, 3)
            for m in range(8):
                put(("dn", i, j, m), wd[m])
    for l in range(2):
        wq = inp["attn_w_qkv"][l]
        qk = wq[:, :1280].reshape(8, 128, 10, 128).transpose(2, 1, 0, 3)
        for m in range(10):
            put(("qk", l, m), qk[m])
        put(("v", l), wq[:, 1280:].reshape(8, 128, 256).transpose(1, 0, 2))
        wo = inp["attn_w_o"][l].reshape(8, 128, 8, 128).transpose(2, 1, 0, 3)
        for m in range(8):
            put(("wo", l, m), wo[m])
        wi = inp["pool_w_in"][l].reshape(8, 128, 8, 128).transpose(2, 1, 0, 3)
        wt = inp["pool_w_out"][l].reshape(8, 128, 8, 128).transpose(2, 1, 0, 3)
        for m in range(8):
            put(("win", l, m), wi[m])
            put(("wout", l, m), wt[m])
        put(("wg", l), inp["pool_w_group"][l].reshape(4, 2, 128, 2, 128).transpose(2, 0, 1, 3, 4))
    return wall


COL_G = 0
COL_PS = 96
COL_QG = 112
COL_KG = 114
COL_QG8 = 116
NS = 120


def pack_small(inp):
    sm = np.zeros((128, NS), np.float32)
    g = inp["norm_gains"].reshape(12, 8, 128)
    sm[:, COL_G:COL_G + 96] = g.transpose(2, 0, 1).reshape(128, 96)
    ps = inp["pool_scale"].reshape(2, 8, 128)
    sm[:, COL_PS:COL_PS + 16] = ps.transpose(2, 0, 1).reshape(128, 16)
    for l in range(2):
        sm[:, COL_QG + l] = np.tile(inp["attn_q_gain"][l], 2)
        sm[:, COL_KG + l] = np.tile(inp["attn_k_gain"][l], 2)
    return sm


def const_mats():
    cm = np.zeros((2, 128, 128), np.float32)
    cm[0, :64, :64] = 1.0
    cm[0, 64:, 64:] = 1.0
    for m in range(128):
        cm[1, m ^ 1, m] = 1.0
    return cm


def rope_tables():
    t = np.arange(SMAX)
    row = (t // 64).astype(np.float32)
    col = (t % 64).astype(np.float32)
    freqs = (np.float32(10000.0) ** (-np.arange(16, dtype=np.float32) / np.float32(16))).astype(np.float32)
    ang = np.concatenate([row[:, None] * freqs, col[:, None] * freqs], -1).astype(np.float32)
    cos = np.cos(ang).astype(np.float32)
    sin = np.sin(ang).astype(np.float32)
    tab = np.zeros((2, 128, SMAX), np.float32)
    for p in range(128):
        d = p % 64
        i = d // 2
        tab[0, p] = cos[:, i]
        tab[1, p] = -sin[:, i] if d % 2 == 0 else sin[:, i]
    return tab


class Op:
    __slots__ = ("eng", "fn", "deps", "sig", "is_dma", "sem", "val", "idx")

    def __init__(self, eng, fn, is_dma=False):
        self.eng = eng
        self.fn = fn
        self.deps = []
        self.sig = False
        self.is_dma = is_dma
        self.sem = None
        self.val = 0


class Rec:
    ENGS = ("sp", "act", "dve", "pool", "pe")

    def __init__(self):
        self.streams = {e: [] for e in self.ENGS}
        self.track = {}
        self.last_dma_on_sem = [None] * NDMASEM
        self.dma_cnt = [0] * NDMASEM
        self.dma_rr = 0
        self.dma_rrq = [0, 0]
        self.pending = {e: [] for e in self.ENGS}

    def _dep(self, op, d, war):
        if d is None or d is op:
            return
        if not d.is_dma and not op.is_dma and d.eng == op.eng:
            if op.eng == "pe":
                return
            if war:
                return
        d.sig = True
        op.deps.append(d)

    def op(self, eng, fn, reads=(), writes=(), is_dma=False, extra=()):
        o = Op(eng, fn, is_dma)
        for d in self.pending[eng]:
            self._dep(o, d, False)
        self.pending[eng] = []
        for d in extra:
            self._dep(o, d, False)
        for k in reads:
            st = self.track.get(k)
            if st is not None:
                self._dep(o, st[0], False)
        for k in writes:
            st = self.track.get(k)
            if st is not None:
                self._dep(o, st[0], True)
                for r in st[1]:
                    self._dep(o, r, True)
        for k in reads:
            st = self.track.get(k)
            if st is None:
                self.track[k] = [None, [o]]
            else:
                st[1].append(o)
        for k in writes:
            self.track[k] = [o, []]
        if is_dma:
            half = NDMASEM // 2
            qi = 0 if eng == "sp" else 1
            s = qi * half + self.dma_rrq[qi]
            self.dma_rrq[qi] = (self.dma_rrq[qi] + 1) % half
            prev = self.last_dma_on_sem[s]
            if prev is not None:
                o.deps.append(prev)
            self.dma_cnt[s] += 1
            o.sem = ("dma", s)
            o.val = 16 * self.dma_cnt[s]
            o.sig = True
            self.last_dma_on_sem[s] = o
        self.streams[eng].append(o)
        return o

    def dma(self, eng, out, in_, reads=(), writes=()):
        return self.op(eng, lambda e: e.dma_start(out=out, in_=in_), reads, writes, is_dma=True)

    def all_deps(self):
        deps = []
        for e in self.ENGS:
            for o in reversed(self.streams[e]):
                if not o.is_dma:
                    deps.append(o)
                    break
        for o in self.last_dma_on_sem:
            if o is not None:
                deps.append(o)
        return deps

    def add_reader(self, keys, o):
        for k in keys:
            st = self.track.get(k)
            if st is None:
                self.track[k] = [None, [o]]
            else:
                st[1].append(o)

    def barrier(self):
        deps = []
        for e in self.ENGS:
            for o in reversed(self.streams[e]):
                if not o.is_dma:
                    deps.append(o)
                    break
        for o in self.last_dma_on_sem:
            if o is not None:
                deps.append(o)
        for e in self.ENGS:
            self.pending[e] = list(deps)
        self.track = {}

    def finalize(self):
        nsem = {}
        for e in self.ENGS:
            ep = 0
            cnt = 0
            for o in self.streams[e]:
                if o.is_dma or not o.sig:
                    continue
                if cnt >= EPOCH:
                    ep += 1
                    cnt = 0
                cnt += 1
                o.sem = (e, ep)
                o.val = cnt
            nsem[e] = ep + 1
        return nsem


def tile_info(seqs):
    out = []
    s0 = 0
    for L in seqs:
        for a in range(0, L, T):
            out.append((s0 + a, s0, L))
        s0 += L
    return out


def build_program(seqs=SEQS_FULL, phases=None):
    NTOK = sum(seqs)
    tiles = tile_info(seqs)
    woff, NW = weight_layout()
    nc = bass.Bass("TRN2", target_bir_lowering=False)

    xT = nc.dram_tensor("xT", [D, NTOK], F32, kind="ExternalInput").ap()
    wall = nc.dram_tensor("wall", [NW], F32, kind="ExternalInput").ap()
    small = nc.dram_tensor("small", [128, NS], F32, kind="ExternalInput").ap()
    cmat = nc.dram_tensor("cmat", [2, 128, 128], F32, kind="ExternalInput").ap()
    rope = nc.dram_tensor("rope", [2, 128, SMAX], F32, kind="ExternalInput").ap()
    yT = nc.dram_tensor("yT", [D, NTOK], F32, kind="ExternalOutput").ap()
    wbf = nc.dram_tensor("wbf", [NW], BF16).ap()
    xs = nc.dram_tensor("xs", [D, NTOK], F32).ap()
    us = nc.dram_tensor("us", [D, NTOK], F32).ap()
    qs = nc.dram_tensor("qs", [D, NTOK], BF16).ap()
    ks = nc.dram_tensor("ks", [256, NTOK], BF16).ap()
    vs = nc.dram_tensor("vs", [NTOK, 512], BF16).ap()
    os_ = nc.dram_tensor("os", [D, NTOK], BF16).ap()

    base = [17408]

    def alloc(name, shape, dt, at=None):
        n = int(np.prod(shape[1:])) * (4 if dt == F32 else 2)
        n = (n + 63) // 64 * 64
        if at is None:
            at = base[0]
            base[0] += n
        assert at + n <= 229376 - 512, (name, at, n)
        return nc.alloc_sbuf_tensor_at(name, list(shape), dt, offset=at)

    SM = alloc("SM", [128, NS], F32)
    CM = alloc("CM", [128, 2, 128], F32)
    ONESB = alloc("ONESB", [128, 128], BF16)
    EPSC = alloc("EPSC", [128, 16], F32)
    vst_at = base[0]
    VST = alloc("VST", [128, 8, 4, 128], BF16)
    VSTF = alloc("VSTF", [128, 4096], BF16, at=vst_at)
    common_end = base[0]
    X = alloc("X", [128, 8, T], F32)
    A = alloc("A", [128, 8, T], BF16)
    B = alloc("B", [128, 2, T + 16, 2], F32)
    H = alloc("H", [128, NFC, T], BF16)
    NW_SLOT = 6
    W = alloc("W", [128, NW_SLOT, 2816], BF16)
    CS = alloc("CS", [128, 2, T], F32)
    SQ = alloc("SQ", [128, 4, 512], BF16)
    SD = alloc("SD", [128, 2, 512], F32)
    RS = alloc("RS", [128, 2, 512], F32)
    SG = alloc("SG", [128, 3, 512], F32)
    tl_end = base[0]
    offs = {}
    cur = common_end
    for nm, nbytes in (("X", 8 * T * 4), ("A", 8 * T * 2), ("B", 2 * (T + 16) * 2 * 4), ("H", NFC * T * 2)):
        offs[nm] = cur
        cur += (nbytes + 63) // 64 * 64
    Bz = alloc("Bz", [128, 8, T], BF16, at=offs["B"])
    TA = alloc("TA", [128, 2, T + 16], F32, at=offs["B"])
    TB = alloc("TB", [128, 2, T + 16], F32, at=offs["B"] + 2 * (T + 16) * 4)
    U = alloc("U", [128, 8, T], F32, at=offs["H"])
    UH = alloc("UH", [128, 8, T + 16], F32, at=offs["H"])
    hq = offs["H"]
    QKST = alloc("QKST", [128, 2, 10, 512], BF16, at=hq); hq += 2 * 10 * 512 * 2
    SQ32 = alloc("SQ32", [128, 2, 512], F32, at=hq); hq += 2 * 512 * 4
    QN = alloc("QN", [128, 2, 512], F32, at=hq); hq += 2 * 512 * 4
    T1 = alloc("T1", [128, 2, 512], F32, at=hq); hq += 2 * 512 * 4
    T2 = alloc("T2", [128, 2, 512], F32, at=hq); hq += 2 * 512 * 4
    assert hq <= offs["H"] + NFC * T * 2
    cur = common_end
    smx = max(seqs)
    KD = alloc("KD", [128, 4, smx], BF16, at=cur); cur += 4 * smx * 2
    VA = alloc("VA", [128, smx // 128, 4, 128], BF16, at=cur); cur += (smx // 128) * 512 * 2
    QZ = alloc("QZ", [128, 2, 2, 512], BF16, at=cur); cur += 2 * 2 * 512 * 2
    PT = alloc("PT", [128, 3, 1024], BF16, at=cur); cur += 3 * 1024 * 2
    OST = alloc("OST", [128, 2, 512], BF16, at=cur); cur += 2 * 512 * 2
    REC = alloc("REC", [128, 2, 512], F32, at=cur); cur += 2 * 512 * 4
    OC = alloc("OC", [128, 2, 2, 512], F32, at=cur); cur += 2 * 2 * 512 * 4
    DC = alloc("DC", [128, 2, 2, 512], F32, at=cur); cur += 2 * 2 * 512 * 4
    assert cur <= 229376 - 512, cur

    PS = nc.alloc_psum_tensor("PS", [128, 8, 512], F32)

    R = Rec()
    bank_rr = [0]

    def nbank():
        b = bank_rr[0]
        bank_rr[0] = (b + 1) % 8
        return b

    ring = {"W": 0, "SQ": 0, "SG": 0}

    def nslot(name, n):
        s = ring[name]
        ring[name] = (s + 1) % n
        return s

    def sl(s):
        return slice(s * 512, (s + 1) * 512)

    HKEYS = [("H", fc, s) for fc in range(NFC) for s in range(NSUB)]

    R.dma("sp", SM[:, :], small[:, :], writes=[("SM",)])
    R.dma("sp", CM[:, :, :], cmat.rearrange("a p c -> p a c"), writes=[("CM",)])
    R.op("dve", lambda e: e.memset(ONESB[:, :], 1.0), writes=[("ONESB",)])
    R.op("dve", lambda e: e.memset(EPSC[:, :], EPS), writes=[("EPSC",)])
    R.op("dve", lambda e: e.tensor_scalar(SM[:, COL_QG8:COL_QG8 + 2], SM[:, COL_QG:COL_QG + 2], 0.125, None, op0=ALU.mult),
         reads=[("SM",)], writes=[("SMq",)])
    nchunk = (NW + CAST_CHUNK - 1) // CAST_CHUNK
    cast_next = [0]

    def cast_upto(n):
        n = min(n, nchunk)
        while cast_next[0] < n:
            c = cast_next[0]
            a, b = c * CAST_CHUNK, min(NW, (c + 1) * CAST_CHUNK)
            R.dma("pool", wbf[a:b].rearrange("(r c) -> r c", c=2048), wall[a:b].rearrange("(r c) -> r c", c=2048),
                  writes=[("wbf", c)])
            cast_next[0] += 1

    if phases is not None and 1 not in phases:
        cast_upto(nchunk)

    def wload(key, L):
        o = woff[key]
        slot = nslot("W", NW_SLOT)
        c0, c1 = o // CAST_CHUNK, (o + 128 * L - 1) // CAST_CHUNK
        R.dma("sp", W[:, slot, 0:L], wbf[o:o + 128 * L].rearrange("(p l) -> p l", p=128),
              reads=[("wbf", c) for c in range(c0, c1 + 1)], writes=[("W", slot)])
        return slot

    def norm(n):
        for s in range(NSUB):
            b = nbank()
            for kc in range(8):
                q = nslot("SQ", 4)
                if kc % 2 == 0:
                    R.op("act", lambda e, kc=kc, q=q, s=s: e.activation(out=SQ[:, q, :], in_=X[:, kc, sl(s)], func=AF.Square),
                         reads=[("X", kc, s)], writes=[("SQ", q)])
                else:
                    R.op("dve", lambda e, kc=kc, q=q, s=s: e.tensor_tensor(out=SQ[:, q, :], in0=X[:, kc, sl(s)], in1=X[:, kc, sl(s)], op=ALU.mult),
                         reads=[("X", kc, s)], writes=[("SQ", q)])
                R.op("pe", lambda e, kc=kc, q=q, b=b: e.matmul(PS[:, b, :], ONESB[:, :], SQ[:, q, :], start=(kc == 0), stop=(kc == 7)),
                     reads=[("SQ", q), ("ONESB",)], writes=[("ps", b)])
            R.op("act", lambda e, b=b, s=s: e.activation(out=SD[:, s, :], in_=PS[:, b, :], func=AF.Ln, scale=1.0 / D, bias=EPSC[:, 0:1]),
                 reads=[("ps", b), ("EPSC",)], writes=[("SD", s)])
            R.op("act", lambda e, s=s: e.activation(out=RS[:, s, :], in_=SD[:, s, :], func=AF.Exp, scale=-0.5),
                 reads=[("SD", s)], writes=[("RS", s)])
            for kc in range(8):
                R.op("dve", lambda e, kc=kc, s=s: e.scalar_tensor_tensor(
                    out=A[:, kc, sl(s)], in0=X[:, kc, sl(s)], scalar=SM[:, COL_G + n * 8 + kc:COL_G + n * 8 + kc + 1],
                    in1=RS[:, s, :], op0=ALU.mult, op1=ALU.mult),
                    reads=[("X", kc, s), ("RS", s), ("SM",)], writes=[("A", kc, s)])

    def ffn(i, j):
        for fc in range(NFC):
            w = wload(("gu", i, j, fc), 2048)
            for s in range(NSUB):
                bg, bu = nbank(), nbank()
                for kc in range(8):
                    R.op("pe", lambda e, kc=kc, w=w, s=s, bg=bg: e.matmul(
                        PS[:, bg, :], W[:, w, kc * 128:(kc + 1) * 128], A[:, kc, sl(s)], start=(kc == 0), stop=(kc == 7)),
                        reads=[("W", w), ("A", kc, s)], writes=[("ps", bg)])
                for kc in range(8):
                    R.op("pe", lambda e, kc=kc, w=w, s=s, bu=bu: e.matmul(
                        PS[:, bu, :], W[:, w, 1024 + kc * 128:1024 + (kc + 1) * 128], A[:, kc, sl(s)], start=(kc == 0), stop=(kc == 7)),
                        reads=[("W", w), ("A", kc, s)], writes=[("ps", bu)])
                g = nslot("SG", 3)
                R.op("act", lambda e, bg=bg, g=g: e.activation(out=SG[:, g, :], in_=PS[:, bg, :], func=AF.Silu),
                     reads=[("ps", bg)], writes=[("SG", g)])
                R.op("dve", lambda e, bu=bu, g=g, fc=fc, s=s: e.tensor_tensor(
                    out=H[:, fc, sl(s)], in0=PS[:, bu, :], in1=SG[:, g, :], op=ALU.mult),
                    reads=[("ps", bu), ("SG", g)], writes=[("H", fc, s)])
        for m in range(8):
            w = wload(("dn", i, j, m), 2816)
            for s in range(NSUB):
                b = nbank()
                for fc in range(NFC):
                    R.op("pe", lambda e, fc=fc, w=w, s=s, b=b: e.matmul(
                        PS[:, b, :], W[:, w, fc * 128:(fc + 1) * 128], H[:, fc, sl(s)], start=(fc == 0), stop=(fc == NFC - 1)),
                        reads=[("W", w), ("H", fc, s)], writes=[("ps", b)])
                R.op("dve", lambda e, m=m, s=s, b=b: e.scalar_tensor_tensor(
                    out=X[:, m, sl(s)], in0=PS[:, b, :], scalar=0.5, in1=X[:, m, sl(s)], op0=ALU.mult, op1=ALU.add),
                    reads=[("ps", b), ("X", m, s)], writes=[("X", m, s)])

    def linear8(keyf, src, srckey, evac):
        for m in range(8):
            w = wload(keyf(m), 1024)
            for s in range(NSUB):
                b = nbank()
                for kc in range(8):
                    R.op("pe", lambda e, kc=kc, w=w, s=s, b=b: e.matmul(
                        PS[:, b, :], W[:, w, kc * 128:(kc + 1) * 128], src[:, kc, sl(s)], start=(kc == 0), stop=(kc == 7)),
                        reads=[("W", w), (srckey, kc, s)], writes=[("ps", b)])
                evac(m, s, b)

    def load_x(src, t0):
        R.dma("pool", X[:, :, :], src[:, t0:t0 + T].rearrange("(kc p) t -> p kc t", p=128),
              reads=[("dX", t0)], writes=[("X", kc, s) for kc in range(8) for s in range(NSUB)])

    def store_x(dst, t0, key):
        return R.dma("pool", dst[:, t0:t0 + T].rearrange("(kc p) t -> p kc t", p=128), X[:, :, :],
                     reads=[("X", kc, s) for kc in range(8) for s in range(NSUB)], writes=[(key, t0)])

    def qkv(l, t0, pos0):
        wv = None
        for s in range(NSUB):
            stages = []
            for step in range(10 + 2):
                if step < 10:
                    m = step
                    w = wload(("qk", l, m), 1024)
                    b = nbank()
                    for kc in range(8):
                        R.op("pe", lambda e, kc=kc, w=w, s=s, b=b: e.matmul(
                            PS[:, b, :], W[:, w, kc * 128:(kc + 1) * 128], A[:, kc, sl(s)], start=(kc == 0), stop=(kc == 7)),
                            reads=[("W", w), ("A", kc, s)], writes=[("ps", b)])
                    z = m % 2
                    R.op("act", lambda e, b=b, z=z: e.activation(out=SQ32[:, z, :], in_=PS[:, b, :], func=AF.Square),
                         reads=[("ps", b)], writes=[("SQ32", z)])
                    stages.append({"b": b, "z": z})
                if 1 <= step < 11:
                    m = step - 1
                    st = stages[m]
                    b, z = st["b"], st["z"]
                    b2 = nbank()
                    R.op("pe", lambda e, b2=b2, z=z: e.matmul(PS[:, b2, :], CM[:, 0, :], SQ32[:, z, :], start=True, stop=True),
                         reads=[("SQ32", z), ("CM",)], writes=[("ps", b2)])
                    R.op("act", lambda e, b2=b2, z=z: e.activation(out=SD[:, z, :], in_=PS[:, b2, :], func=AF.Ln, scale=1.0 / 64, bias=EPSC[:, 0:1]),
                         reads=[("ps", b2), ("EPSC",)], writes=[("SD", z)])
                    R.op("act", lambda e, z=z: e.activation(out=RS[:, z, :], in_=SD[:, z, :], func=AF.Exp, scale=-0.5),
                         reads=[("SD", z)], writes=[("RS", z)])
                    gc = (COL_QG8 + l) if m < 8 else (COL_KG + l)
                    R.op("dve", lambda e, b=b, z=z, gc=gc: e.scalar_tensor_tensor(
                        out=QN[:, z, :], in0=PS[:, b, :], scalar=SM[:, gc:gc + 1], in1=RS[:, z, :], op0=ALU.mult, op1=ALU.mult),
                        reads=[("ps", b), ("RS", z), ("SM",), ("SMq",)], writes=[("QN", z)])
                if 2 <= step < 12:
                    m = step - 2
                    z = stages[m]["z"]
                    b3 = nbank()
                    R.op("pe", lambda e, b3=b3, z=z: e.matmul(PS[:, b3, :], CM[:, 1, :], QN[:, z, :], start=True, stop=True),
                         reads=[("QN", z), ("CM",)], writes=[("ps", b3)])
                    R.op("dve", lambda e, z=z, s=s: e.tensor_tensor(out=T1[:, z, :], in0=QN[:, z, :], in1=CS[:, 0, sl(s)], op=ALU.mult),
                         reads=[("QN", z), ("CS",)], writes=[("T1", z)])
                    R.op("dve", lambda e, z=z, s=s, b3=b3: e.tensor_tensor(out=T2[:, z, :], in0=PS[:, b3, :], in1=CS[:, 1, sl(s)], op=ALU.mult),
                         reads=[("ps", b3), ("CS",)], writes=[("T2", z)])
                    R.op("pool", lambda e, z=z, s=s, m=m: e.tensor_tensor(out=QKST[:, s, m, :], in0=T1[:, z, :], in1=T2[:, z, :], op=ALU.add),
                         reads=[("T1", z), ("T2", z)], writes=[("QKST", s, m)])
            wv = wload(("v", l), 2048)
            for blk in range(4):
                b = nbank()
                for kc in range(8):
                    R.op("pe", lambda e, kc=kc, wv=wv, s=s, b=b, blk=blk: e.matmul(
                        PS[:, b, 0:256], A[:, kc, s * 512 + blk * 128:s * 512 + (blk + 1) * 128],
                        W[:, wv, kc * 256:(kc + 1) * 256], start=(kc == 0), stop=(kc == 7)),
                        reads=[("W", wv), ("A", kc, s)], writes=[("ps", b)])
                R.op("act", lambda e, s=s, b=b, blk=blk: e.activation(
                    out=VST[:, s * 4 + blk, :, 0:64], in_=PS[:, b, 0:256].rearrange("p (g d) -> p g d", g=4), func=AF.Copy),
                    reads=[("ps", b)], writes=[("VST", s, blk)])
            ts = t0 + s * 512
            d1 = R.dma("pool", qs[:, ts:ts + 512].rearrange("(m p) t -> p m t", p=128), QKST[:, s, 0:8, :],
                       reads=[("QKST", s, m) for m in range(8)], writes=[("qs", ts)])
            d2 = R.dma("pool", ks[:, ts:ts + 512].rearrange("(m p) t -> p m t", p=128), QKST[:, s, 8:10, :],
                       reads=[("QKST", s, m) for m in (8, 9)], writes=[("ks", ts)])
            R.dma("pool", vs[ts:ts + 512, :].rearrange("(b p) f -> p b f", p=128),
                  VSTF[:, s * 2048:(s + 1) * 2048].rearrange("p (b f) -> p b f", b=4),
                  reads=[("VST", s, blk) for blk in range(4)], writes=[("vs", ts)])
            R.add_reader(HKEYS, d1)
            R.add_reader(HKEYS, d2)

    def wo(l, t0):
        R.dma("pool", Bz[:, :, :], os_[:, t0:t0 + T].rearrange("(kc p) t -> p kc t", p=128),
              reads=[("os", t0)], writes=[("B", kc, s) for kc in range(8) for s in range(NSUB)])

        def ev(m, s, b):
            R.op("dve", lambda e: e.tensor_tensor(out=X[:, m, sl(s)], in0=PS[:, b, :], in1=X[:, m, sl(s)], op=ALU.add),
                 reads=[("ps", b), ("X", m, s)], writes=[("X", m, s)])
        linear8(lambda m: ("wo", l, m), Bz, "B", ev)

    def pool_in(l, t0):
        def ev(m, s, b):
            R.op("act", lambda e: e.activation(out=U[:, m, sl(s)], in_=PS[:, b, :], func=AF.Copy),
                 reads=[("ps", b)], writes=[("U", m, s)])
        linear8(lambda m: ("win", l, m), A, "A", ev)
        d = R.dma("pool", us[:, t0:t0 + T].rearrange("(kc p) t -> p kc t", p=128), U[:, :, :],
                  reads=[("U", m, s) for m in range(8) for s in range(NSUB)], writes=[("us", t0)])
        R.add_reader(HKEYS, d)

    def pool_mix(l, t0, s0, L):
        first = (t0 == s0)
        last = (t0 + T == s0 + L)
        lo = t0 - (0 if first else 8)
        hi = t0 + T + (0 if last else 8)
        c0 = 8 - (t0 - lo)
        ad = R.all_deps()
        o = R.op("pool", lambda e: e.dma_start(out=UH[:, :, c0:c0 + (hi - lo)], in_=us[:, lo:hi].rearrange("(kc p) t -> p kc t", p=128)),
                 reads=[("us", "all")], writes=[("UH",)], is_dma=True, extra=ad)
        if first:
            R.op("dve", lambda e: e.memset(UH[:, :, 0:8], 0.0), writes=[("UHl",)], extra=ad)
        if last:
            R.op("dve", lambda e: e.memset(UH[:, :, T + 8:T + 16], 0.0), writes=[("UHr",)], extra=ad)
        NCOL = T + 16
        for g, w in enumerate(POOLW):
            half = w // 2
            ch = slice(2 * g, 2 * g + 2)
            src = UH
            bufs = [TA, TB]
            n = NCOL
            k = 1
            cur = None
            idx = 0
            while k < w:
                n = n - k
                dst = bufs[idx % 2]
                if cur is None:
                    R.op("dve", lambda e, dst=dst, n=n, k=k, ch=ch: e.tensor_tensor(
                        out=dst[:, :, 0:n], in0=UH[:, ch, 0:n], in1=UH[:, ch, k:k + n], op=ALU.add),
                        reads=[("UH",), ("UHl",), ("UHr",)], writes=[("TT", idx % 2)])
                else:
                    R.op("dve", lambda e, dst=dst, cur=cur, n=n, k=k: e.tensor_tensor(
                        out=dst[:, :, 0:n], in0=cur[:, :, 0:n], in1=cur[:, :, k:k + n], op=ALU.add),
                        reads=[("TT", (idx - 1) % 2)], writes=[("TT", idx % 2)])
                cur = dst
                idx += 1
                k *= 2
            sb = 8 - half
            tk = ("TT", (idx - 1) % 2)
            R.op("dve", lambda e, cur=cur, sb=sb, w=w, ch=ch: e.scalar_tensor_tensor(
                out=A[:, ch, :], in0=cur[:, :, sb:sb + T], scalar=1.0 / w, in1=UH[:, ch, 8:8 + T], op0=ALU.mult, op1=ALU.subtract),
                reads=[tk, ("UH",)], writes=[("A", kc, s) for kc in (2 * g, 2 * g + 1) for s in range(NSUB)])
            fix = []
            if first:
                fix += [(t, t + half) for t in range(half)]
            if last:
                fix += [(t, (T - t) + half) for t in range(T - half + 1, T)]
            for (t, cnt) in fix:
                R.op("dve", lambda e, cur=cur, sb=sb, t=t, cnt=cnt, ch=ch: e.scalar_tensor_tensor(
                    out=A[:, ch, t:t + 1], in0=cur[:, :, sb + t:sb + t + 1], scalar=1.0 / cnt, in1=UH[:, ch, 8 + t:9 + t],
                    op0=ALU.mult, op1=ALU.subtract),
                    reads=[tk, ("UH",)], writes=[("A", kc, t // 512) for kc in (2 * g, 2 * g + 1)])
        wg = wload(("wg", l), 2048)
        for g in range(4):
            for m2 in range(2):
                for s in range(NSUB):
                    b = nbank()
                    for k2 in range(2):
                        c = ((g * 2 + k2) * 2 + m2) * 128
                        R.op("pe", lambda e, c=c, wg=wg, g=g, k2=k2, s=s, b=b: e.matmul(
                            PS[:, b, :], W[:, wg, c:c + 128], A[:, 2 * g + k2, sl(s)], start=(k2 == 0), stop=(k2 == 1)),
                            reads=[("W", wg), ("A", 2 * g + k2, s)], writes=[("ps", b)])
                    R.op("act", lambda e, g=g, m2=m2, s=s, b=b: e.activation(out=Bz[:, 2 * g + m2, sl(s)], in_=PS[:, b, :], func=AF.Copy),
                         reads=[("ps", b)], writes=[("B", 2 * g + m2, s)] + ([("TT", 0), ("TT", 1)] if (g, m2, s) == (0, 0, 0) else []))

        def ev(m, s, b):
            R.op("dve", lambda e: e.scalar_tensor_tensor(
                out=X[:, m, sl(s)], in0=PS[:, b, :], scalar=SM[:, COL_PS + l * 8 + m:COL_PS + l * 8 + m + 1], in1=X[:, m, sl(s)],
                op0=ALU.mult, op1=ALU.add),
                reads=[("ps", b), ("X", m, s), ("SM",)], writes=[("X", m, s)])
        linear8(lambda m: ("wout", l, m), Bz, "B", ev)

    def attention():
        R.op("dve", lambda e: e.memset(QZ[:, :, :, :], 0.0), writes=[("QZ", 0), ("QZ", 1)])
        s0 = 0
        it = 0
        for L in seqs:
            nkt = L // 128
            for g in range(4):
                for hb in (0, 64):
                    R.dma("sp", KD[hb:hb + 64, g, 0:L], ks[g * 64:(g + 1) * 64, s0:s0 + L], reads=[("ks", "all")], writes=[("KD", g)])
            step = 16
            for k0 in range(0, nkt, step):
                k1 = min(nkt, k0 + step)
                R.dma("sp", VA[:, k0:k1, :, :].rearrange("p k g d -> p k (g d)"),
                      vs[s0 + k0 * 128:s0 + k1 * 128, :].rearrange("(k p) f -> p k f", p=128),
                      reads=[("vs", "all")], writes=[("VA",)])
            iters = [(g, hp, qt) for g in range(4) for hp in range(2) for qt in range(L // 512)]

            def load_q(k, s0=s0):
                g, hp, qt = iters[k]
                c = 2 * g + hp
                tq = s0 + qt * 512
                for r in range(2):
                    R.dma("sp", QZ[r * 64:(r + 1) * 64, (it0 + k) % 2, r, :], qs[c * 128 + r * 64:c * 128 + (r + 1) * 64, tq:tq + 512],
                          reads=[("qs", "all")], writes=[("QZ", (it0 + k) % 2)])

            it0 = it
            load_q(0)
            for k, (g, hp, qt) in enumerate(iters):
                c = 2 * g + hp
                tq = s0 + qt * 512
                qsl = it % 2
                it += 1
                if k + 1 < len(iters):
                    load_q(k + 1)
                LA = 2

                def QK(i, g=g, qsl=qsl):
                    j = i % 3
                    for r in range(2):
                        sb = 2 + 2 * j + r
                        R.op("pe", lambda e, r=r, sb=sb: e.matmul(PS[:, sb, :], KD[:, g, i * 128:(i + 1) * 128],
                                                                 QZ[:, qsl, r, :], start=True, stop=True),
                             reads=[("KD", g), ("QZ", qsl)], writes=[("ps", sb)])
                    R.op("act", lambda e: e.activation(out=PT[:, j, :], in_=PS[:, 2 + 2 * j:4 + 2 * j, :].rearrange("p a b -> p (a b)"), func=AF.Exp),
                         reads=[("ps", 2 + 2 * j), ("ps", 3 + 2 * j)], writes=[("PT", j)])

                def PV(i, g=g, nkt=nkt):
                    j = i % 3
                    for r in range(2):
                        R.op("pe", lambda e, r=r: e.matmul(PS[:, r, :], VA[:, i, g, :], PT[:, j, r * 512:(r + 1) * 512],
                                                           start=(i == 0), stop=(i == nkt - 1)),
                             reads=[("VA",), ("PT", j)], writes=[("ps", r)])

                for i in range(nkt + LA):
                    if i < nkt:
                        QK(i)
                    if i >= LA:
                        PV(i - LA)
                for r in range(2):
                    R.op("dve", lambda e, r=r, qsl=qsl: e.tensor_copy(out=OC[0:64, qsl, r, :], in_=PS[0:64, r, :]),
                         reads=[("ps", r)], writes=[("OC", qsl, r)])
                    R.op("dve", lambda e, r=r, qsl=qsl: e.tensor_copy(out=DC[0:64, qsl, r, :], in_=PS[64:128, r, :]),
                         reads=[("ps", r)], writes=[("DC", qsl, r)])
                for r in range(2):
                    R.op("dve", lambda e, r=r, qsl=qsl: e.reciprocal(out=REC[0:64, r, :], in_=DC[0:64, qsl, r, :]),
                         reads=[("DC", qsl, r)], writes=[("REC", r)])
                    R.op("dve", lambda e, r=r, qsl=qsl: e.tensor_tensor(
                        out=OST[r * 64:(r + 1) * 64, qsl, :], in0=OC[0:64, qsl, r, :], in1=REC[0:64, r, :], op=ALU.mult),
                        reads=[("OC", qsl, r), ("REC", r)], writes=[("OST", qsl)])
                R.dma("pool", os_[c * 128:(c + 1) * 128, tq:tq + 512], OST[:, qsl, :],
                      reads=[("OST", qsl)], writes=[("os", tq)])
            s0 += L

    def phase_ok(p):
        return phases is None or p in phases

    def vst_ones():
        R.op("dve", lambda e: e.memset(VSTF[:, :], 1.0), writes=[("VST", s, blk) for s in range(NSUB) for blk in range(4)])

    vst_ones()
    def nxt(k):
        return tiles[k + 1][0] if k + 1 < len(tiles) else None

    def load_cs(pos0):
        R.dma("pool", CS[:, :, :], rope[:, :, pos0:pos0 + T].rearrange("a p t -> p a t"), writes=[("CS",)])

    if phase_ok(1):
        load_x(xT, tiles[0][0])
        n_p1 = (woff[("wo", 0, 0)] + CAST_CHUNK - 1) // CAST_CHUNK
        cast_upto(n_p1)
        per_tile = (nchunk - n_p1 + len(tiles) - 1) // len(tiles) + 1
        for k, (t0, s0, L) in enumerate(tiles):
            load_cs(t0 - s0)
            norm(0); ffn(0, 0)
            norm(1)
            store_x(xs, t0, "xs")
            if nxt(k) is not None:
                load_x(xT, nxt(k))
            cast_upto(cast_next[0] + per_tile)
            qkv(0, t0, t0 - s0)
        cast_upto(nchunk)
        R.barrier()
    if phase_ok(2):
        attention()
        R.barrier()
    if phase_ok(3):
        load_x(xs, tiles[0][0])
        for k, (t0, s0, L) in enumerate(tiles):
            wo(0, t0)
            norm(2); ffn(0, 1)
            norm(3); ffn(1, 0)
            norm(4)
            store_x(xs, t0, "xs")
            if nxt(k) is not None:
                load_x(xs, nxt(k))
            pool_in(0, t0)
        R.barrier()
    if phase_ok(4):
        load_x(xs, tiles[0][0])
        for k, (t0, s0, L) in enumerate(tiles):
            load_cs(t0 - s0)
            pool_mix(0, t0, s0, L)
            norm(5); ffn(1, 1)
            norm(6); ffn(2, 0)
            norm(7)
            store_x(xs, t0, "xs")
            if nxt(k) is not None:
                load_x(xs, nxt(k))
            qkv(1, t0, t0 - s0)
        R.barrier()
    if phase_ok(5):
        attention()
        R.barrier()
    if phase_ok(6):
        load_x(xs, tiles[0][0])
        for k, (t0, s0, L) in enumerate(tiles):
            wo(1, t0)
            norm(8); ffn(2, 1)
            norm(9); ffn(3, 0)
            norm(10)
            store_x(xs, t0, "xs")
            if nxt(k) is not None:
                load_x(xs, nxt(k))
            pool_in(1, t0)
        R.barrier()
    if phase_ok(7):
        load_x(xs, tiles[0][0])
        for k, (t0, s0, L) in enumerate(tiles):
            pool_mix(1, t0, s0, L)
            norm(11); ffn(3, 1)
            for m in range(8):
                R.dma("pool", yT[m * 128:(m + 1) * 128, t0:t0 + T], X[:, m, :],
                      reads=[("X", m, s) for s in range(NSUB)], writes=[("y", t0, m)])
            if nxt(k) is not None:
                for m in range(8):
                    R.dma("pool", X[:, m, :], xs[m * 128:(m + 1) * 128, nxt(k):nxt(k) + T],
                          reads=[("dX", nxt(k))], writes=[("X", m, s) for s in range(NSUB)])
        R.barrier()
    elif phases is not None:
        for (t0, s0, L) in tiles:
            load_x(xs, t0)
            store_x(yT, t0, "y")
        R.barrier()

    nsem = R.finalize()
    from contextlib import ExitStack
    with ExitStack() as es:
        sems = {}
        for e in Rec.ENGS:
            for ep in range(nsem[e]):
                sems[(e, ep)] = es.enter_context(nc.semaphore(f"s_{e}_{ep}"))
        for i in range(NDMASEM):
            sems[("dma", i)] = es.enter_context(nc.semaphore(f"s_dma_{i}"))
        block = es.enter_context(nc.Block())

        def replay(name, eng):
            waited = {}
            for o in R.streams[name]:
                for d in o.deps:
                    if waited.get(d.sem, 0) >= d.val:
                        continue
                    eng.wait_ge(sems[d.sem], d.val)
                    waited[d.sem] = d.val
                ins = o.fn(eng)
                if o.sig:
                    ins.then_inc(sems[o.sem], 16 if o.is_dma else 1)
            for d in R.pending[name]:
                if d.sem is None or waited.get(d.sem, 0) >= d.val:
                    continue
                eng.wait_ge(sems[d.sem], d.val)
                waited[d.sem] = d.val

        @block.sync
        def _(e):
            replay("sp", e)

        @block.scalar
        def _(e):
            replay("act", e)

        @block.vector
        def _(e):
            replay("dve", e)

        @block.gpsimd
        def _(e):
            replay("pool", e)

        @block.tensor
        def _(e):
            replay("pe", e)
    return nc


def make_in_maps(inputs, seqs, xts):
    wall = pack_weights(inputs)
    sm = pack_small(inputs)
    cm = const_mats()
    rp = rope_tables()
    return [{"xT": x, "wall": wall, "small": sm, "cmat": cm, "rope": rp} for x in xts]


def kernel(**inputs):
    inputs = {k: np.asarray(v) for k, v in inputs.items()}
    xp = inputs["x_prompt"]
    xsm = inputs["x_sample"]
    n = 8
    xts = []
    for c in range(n):
        rows = np.concatenate([xp[2 * c], xp[2 * c + 1], xsm[c]], axis=0)
        xts.append(np.ascontiguousarray(rows.T))
    nc = build_program(SEQS_FULL)
    in_maps = make_in_maps(inputs, SEQS_FULL, xts)
    res = run_bass_kernel_spmd(nc, in_maps, core_ids=list(range(n)))
    yp = np.empty_like(xp)
    ysm = np.empty_like(xsm)
    for c in range(n):
        y = np.asarray(res.results[c]["yT"]).T
        yp[2 * c] = y[0:4096]
        yp[2 * c + 1] = y[4096:8192]
        ysm[c] = y[8192:16384]
    return (yp, ysm)
```

```python
import numpy as np
import concourse.bass as bass
import concourse.mybir as mybir
from concourse.bass_utils import run_bass_kernel_spmd

F32 = mybir.dt.float32
BF16 = mybir.dt.bfloat16
ALU = mybir.AluOpType
AF = mybir.ActivationFunctionType

D = 1024
DFF = 2816
NFC = 22
NKC = 8
T = 1024
NSUB = T // 512
EPS = 1e-6
DEPTH = 4
POOLW = (2, 4, 8, 16)
SMAX = 8192
SEQS_FULL = (4096, 4096, 8192)
EPOCH = 16000
NDMASEM = 24
CAST_CHUNK = 1 << 21


def weight_layout():
    off = {}
    cur = 0

    def add(key, n):
        nonlocal cur
        off[key] = cur
        cur += n

    def ffn(i, j):
        for fc in range(NFC):
            add(("gu", i, j, fc), 128 * 2048)
        for m in range(8):
            add(("dn", i, j, m), 128 * 2816)

    def attn_qkv(l):
        for m in range(10):
            add(("qk", l, m), 128 * 1024)
        add(("v", l), 128 * 2048)

    def attn_o(l):
        for m in range(8):
            add(("wo", l, m), 128 * 1024)

    def pool_in(l):
        for m in range(8):
            add(("win", l, m), 128 * 1024)

    def pool_rest(l):
        add(("wg", l), 128 * 2048)
        for m in range(8):
            add(("wout", l, m), 128 * 1024)

    ffn(0, 0); attn_qkv(0); attn_o(0); ffn(0, 1)
    ffn(1, 0); pool_in(0); pool_rest(0); ffn(1, 1)
    ffn(2, 0); attn_qkv(1); attn_o(1); ffn(2, 1)
    ffn(3, 0); pool_in(1); pool_rest(1); ffn(3, 1)
    return off, cur


def pack_weights(inp):
    off, total = weight_layout()
    wall = np.empty(total, np.float32)

    def put(key, arr):
        a = np.ascontiguousarray(arr, dtype=np.float32).reshape(-1)
        wall[off[key]:off[key] + a.size] = a

    for i in range(DEPTH):
        for j in range(2):
            wg = inp["ffn_w_gate"][i, j].reshape(8, 128, NFC, 128)
            wu = inp["ffn_w_up"][i, j].reshape(8, 128, NFC, 128)
            gu = np.stack([wg, wu], 0).transpose(3, 2, 0, 1, 4)
            for fc in range(NFC):
                put(("gu", i, j, fc), gu[fc])
            wd = inp["ffn_w_down"][i, j].reshape(NFC, 128, 8, 128).transpose(2, 1, 0, 3)
            for m in range(8):
                put(("dn", i, j, m), wd[m])
    for l in range(2):
        wq = inp["attn_w_qkv"][l]
        qk = wq[:, :1280].reshape(8, 128, 10, 128).transpose(2, 1, 0, 3)
        for m in range(10):
            put(("qk", l, m), qk[m])
        put(("v", l), wq[:, 1280:].reshape(8, 128, 256).transpose(1, 0, 2))
        wo = inp["attn_w_o"][l].reshape(8, 128, 8, 128).transpose(2, 1, 0, 3)
        for m in range(8):
            put(("wo", l, m), wo[m])
        wi = inp["pool_w_in"][l].reshape(8, 128, 8, 128).transpose(2, 1, 0, 3)
        wt = inp["pool_w_out"][l].reshape(8, 128, 8, 128).transpose(2, 1, 0, 3)
        for m in range(8):
            put(("win", l, m), wi[m])
            put(("wout", l, m), wt[m])
        put(("wg", l), inp["pool_w_group"][l].reshape(4, 2, 128, 2, 128).transpose(2, 0, 1, 3, 4))
    return wall


COL_G = 0
COL_PS = 96
COL_QG = 112
COL_KG = 114
COL_QG8 = 116
NS = 120


def pack_small(inp):
    sm = np.zeros((128, NS), np.float32)
    g = inp["norm_gains"].reshape(12, 8, 128)
    sm[:, COL_G:COL_G + 96] = g.transpose(2, 0, 1).reshape(128, 96)
    ps = inp["pool_scale"].reshape(2, 8, 128)
    sm[:, COL_PS:COL_PS + 16] = ps.transpose(2, 0, 1).reshape(128, 16)
    for l in range(2):
        sm[:, COL_QG + l] = np.tile(inp["attn_q_gain"][l], 2)
        sm[:, COL_KG + l] = np.tile(inp["attn_k_gain"][l], 2)
    return sm


def const_mats():
    cm = np.zeros((2, 128, 128), np.float32)
    cm[0, :64, :64] = 1.0
    cm[0, 64:, 64:] = 1.0
    for m in range(128):
        cm[1, m ^ 1, m] = 1.0
    return cm


def rope_tables():
    t = np.arange(SMAX)
    row = (t // 64).astype(np.float32)
    col = (t % 64).astype(np.float32)
    freqs = (np.float32(10000.0) ** (-np.arange(16, dtype=np.float32) / np.float32(16))).astype(np.float32)
    ang = np.concatenate([row[:, None] * freqs, col[:, None] * freqs], -1).astype(np.float32)
    cos = np.cos(ang).astype(np.float32)
    sin = np.sin(ang).astype(np.float32)
    tab = np.zeros((2, 128, SMAX), np.float32)
    for p in range(128):
        d = p % 64
        i = d // 2
        tab[0, p] = cos[:, i]
        tab[1, p] = -sin[:, i] if d % 2 == 0 else sin[:, i]
    return tab


class Op:
    __slots__ = ("eng", "fn", "deps", "sig", "is_dma", "sem", "val", "idx")

    def __init__(self, eng, fn, is_dma=False):
        self.eng = eng
        self.fn = fn
        self.deps = []
        self.sig = False
        self.is_dma = is_dma
        self.sem = None
        self.val = 0


class Rec:
    ENGS = ("sp", "act", "dve", "pool", "pe")

    def __init__(self):
        self.streams = {e: [] for e in self.ENGS}
        self.track = {}
        self.last_dma_on_sem = [None] * NDMASEM
        self.dma_cnt = [0] * NDMASEM
        self.dma_rr = 0
        self.dma_rrq = [0, 0]
        self.pending = {e: [] for e in self.ENGS}

    def _dep(self, op, d, war):
        if d is None or d is op:
            return
        if not d.is_dma and not op.is_dma and d.eng == op.eng:
            if op.eng == "pe":
                return
            if war:
                return
        d.sig = True
        op.deps.append(d)

    def op(self, eng, fn, reads=(), writes=(), is_dma=False, extra=()):
        o = Op(eng, fn, is_dma)
        for d in self.pending[eng]:
            self._dep(o, d, False)
        self.pending[eng] = []
        for d in extra:
            self._dep(o, d, False)
        for k in reads:
            st = self.track.get(k)
            if st is not None:
                self._dep(o, st[0], False)
        for k in writes:
            st = self.track.get(k)
            if st is not None:
                self._dep(o, st[0], True)
                for r in st[1]:
                    self._dep(o, r, True)
        for k in reads:
            st = self.track.get(k)
            if st is None:
                self.track[k] = [None, [o]]
            else:
                st[1].append(o)
        for k in writes:
            self.track[k] = [o, []]
        if is_dma:
            half = NDMASEM // 2
            qi = 0 if eng == "sp" else 1
            s = qi * half + self.dma_rrq[qi]
            self.dma_rrq[qi] = (self.dma_rrq[qi] + 1) % half
            prev = self.last_dma_on_sem[s]
            if prev is not None:
                o.deps.append(prev)
            self.dma_cnt[s] += 1
            o.sem = ("dma", s)
            o.val = 16 * self.dma_cnt[s]
            o.sig = True
            self.last_dma_on_sem[s] = o
        self.streams[eng].append(o)
        return o

    def dma(self, eng, out, in_, reads=(), writes=()):
        return self.op(eng, lambda e: e.dma_start(out=out, in_=in_), reads, writes, is_dma=True)

    def all_deps(self):
        deps = []
        for e in self.ENGS:
            for o in reversed(self.streams[e]):
                if not o.is_dma:
                    deps.append(o)
                    break
        for o in self.last_dma_on_sem:
            if o is not None:
                deps.append(o)
        return deps

    def add_reader(self, keys, o):
        for k in keys:
            st = self.track.get(k)
            if st is None:
                self.track[k] = [None, [o]]
            else:
                st[1].append(o)

    def barrier(self):
        deps = []
        for e in self.ENGS:
            for o in reversed(self.streams[e]):
                if not o.is_dma:
                    deps.append(o)
                    break
        for o in self.last_dma_on_sem:
            if o is not None:
                deps.append(o)
        for e in self.ENGS:
            self.pending[e] = list(deps)
        self.track = {}

    def finalize(self):
        nsem = {}
        for e in self.ENGS:
            ep = 0
            cnt = 0
            for o in self.streams[e]:
                if o.is_dma or not o.sig:
                    continue
                if cnt >= EPOCH:
                    ep += 1
                    cnt = 0
                cnt += 1
                o.sem = (e, ep)
                o.val = cnt
            nsem[e] = ep + 1
        return nsem


def tile_info(seqs):
    out = []
    s0 = 0
    for L in seqs:
        for a in range(0, L, T):
            out.append((s0 + a, s0, L))
        s0 += L
    return out


def build_program(seqs=SEQS_FULL, phases=None):
    NTOK = sum(seqs)
    tiles = tile_info(seqs)
    woff, NW = weight_layout()
    nc = bass.Bass("TRN2", target_bir_lowering=False)

    xT = nc.dram_tensor("xT", [D, NTOK], F32, kind="ExternalInput").ap()
    wall = nc.dram_tensor("wall", [NW], F32, kind="ExternalInput").ap()
    small = nc.dram_tensor("small", [128, NS], F32, kind="ExternalInput").ap()
    cmat = nc.dram_tensor("cmat", [2, 128, 128], F32, kind="ExternalInput").ap()
    rope = nc.dram_tensor("rope", [2, 128, SMAX], F32, kind="ExternalInput").ap()
    yT = nc.dram_tensor("yT", [D, NTOK], F32, kind="ExternalOutput").ap()
    wbf = nc.dram_tensor("wbf", [NW], BF16).ap()
    xs = nc.dram_tensor("xs", [D, NTOK], F32).ap()
    us = nc.dram_tensor("us", [D, NTOK], F32).ap()
    qs = nc.dram_tensor("qs", [D, NTOK], BF16).ap()
    ks = nc.dram_tensor("ks", [256, NTOK], BF16).ap()
    vs = nc.dram_tensor("vs", [NTOK, 512], BF16).ap()
    os_ = nc.dram_tensor("os", [D, NTOK], BF16).ap()

    base = [17408]

    def alloc(name, shape, dt, at=None):
        n = int(np.prod(shape[1:])) * (4 if dt == F32 else 2)
        n = (n + 63) // 64 * 64
        if at is None:
            at = base[0]
            base[0] += n
        assert at + n <= 229376 - 512, (name, at, n)
        return nc.alloc_sbuf_tensor_at(name, list(shape), dt, offset=at)

    SM = alloc("SM", [128, NS], F32)
    CM = alloc("CM", [128, 2, 128], F32)
    ONESB = alloc("ONESB", [128, 128], BF16)
    EPSC = alloc("EPSC", [128, 16], F32)
    vst_at = base[0]
    VST = alloc("VST", [128, 8, 4, 128], BF16)
    VSTF = alloc("VSTF", [128, 4096], BF16, at=vst_at)
    common_end = base[0]
    X = alloc("X", [128, 8, T], F32)
    A = alloc("A", [128, 8, T], BF16)
    B = alloc("B", [128, 2, T + 16, 2], F32)
    H = alloc("H", [128, NFC, T], BF16)
    NW_SLOT = 5
    W = alloc("W", [128, NW_SLOT, 2816], BF16)
    CS = alloc("CS", [128, 2, T], F32)
    SQ = alloc("SQ", [128, 4, 512], BF16)
    SD = alloc("SD", [128, 2, 512], F32)
    RS = alloc("RS", [128, 2, 512], F32)
    SG = alloc("SG", [128, 3, 512], F32)
    UH = alloc("UH", [128, 8, T + 16], F32)
    tl_end = base[0]
    offs = {}
    cur = common_end
    for nm, nbytes in (("X", 8 * T * 4), ("A", 8 * T * 2), ("B", 2 * (T + 16) * 2 * 4), ("H", NFC * T * 2)):
        offs[nm] = cur
        cur += (nbytes + 63) // 64 * 64
    Bz = alloc("Bz", [128, 8, T], BF16, at=offs["B"])
    TA = alloc("TA", [128, 2, T + 16], F32, at=offs["B"])
    TB = alloc("TB", [128, 2, T + 16], F32, at=offs["B"] + 2 * (T + 16) * 4)
    U = alloc("U", [128, 8, T], F32, at=offs["H"])
    hq = offs["H"]
    QKST = alloc("QKST", [128, 2, 10, 512], BF16, at=hq); hq += 2 * 10 * 512 * 2
    SQ32 = alloc("SQ32", [128, 2, 512], F32, at=hq); hq += 2 * 512 * 4
    QN = alloc("QN", [128, 2, 512], F32, at=hq); hq += 2 * 512 * 4
    T1 = alloc("T1", [128, 2, 512], F32, at=hq); hq += 2 * 512 * 4
    T2 = alloc("T2", [128, 2, 512], F32, at=hq); hq += 2 * 512 * 4
    assert hq <= offs["H"] + NFC * T * 2
    cur = common_end
    smx = max(seqs)
    KD = alloc("KD", [128, 4, smx], BF16, at=cur); cur += 4 * smx * 2
    VA = alloc("VA", [128, smx // 128, 4, 128], BF16, at=cur); cur += (smx // 128) * 512 * 2
    QZ = alloc("QZ", [128, 2, 2, 512], BF16, at=cur); cur += 2 * 2 * 512 * 2
    PT = alloc("PT", [128, 3, 1024], BF16, at=cur); cur += 3 * 1024 * 2
    OST = alloc("OST", [128, 2, 512], BF16, at=cur); cur += 2 * 512 * 2
    REC = alloc("REC", [128, 2, 512], F32, at=cur); cur += 2 * 512 * 4
    OC = alloc("OC", [128, 2, 2, 512], F32, at=cur); cur += 2 * 2 * 512 * 4
    DC = alloc("DC", [128, 2, 2, 512], F32, at=cur); cur += 2 * 2 * 512 * 4
    assert cur <= 229376 - 512, cur

    PS = nc.alloc_psum_tensor("PS", [128, 8, 512], F32)

    R = Rec()
    bank_rr = [0]

    def nbank():
        b = bank_rr[0]
        bank_rr[0] = (b + 1) % 8
        return b

    ring = {"W": 0, "SQ": 0, "SG": 0}

    def nslot(name, n):
        s = ring[name]
        ring[name] = (s + 1) % n
        return s

    def sl(s):
        return slice(s * 512, (s + 1) * 512)

    HKEYS = [("H", fc, s) for fc in range(NFC) for s in range(NSUB)]

    R.dma("sp", SM[:, :], small[:, :], writes=[("SM",)])
    R.dma("sp", CM[:, :, :], cmat.rearrange("a p c -> p a c"), writes=[("CM",)])
    R.op("dve", lambda e: e.memset(ONESB[:, :], 1.0), writes=[("ONESB",)])
    R.op("dve", lambda e: e.memset(EPSC[:, :], EPS), writes=[("EPSC",)])
    R.op("dve", lambda e: e.tensor_scalar(SM[:, COL_QG8:COL_QG8 + 2], SM[:, COL_QG:COL_QG + 2], 0.125, None, op0=ALU.mult),
         reads=[("SM",)], writes=[("SMq",)])
    nchunk = (NW + CAST_CHUNK - 1) // CAST_CHUNK
    for c in range(nchunk):
        a, b = c * CAST_CHUNK, min(NW, (c + 1) * CAST_CHUNK)
        R.dma("pool", wbf[a:b].rearrange("(r c) -> r c", c=2048), wall[a:b].rearrange("(r c) -> r c", c=2048),
              writes=[("wbf", c)])

    def wload(key, L):
        o = woff[key]
        slot = nslot("W", NW_SLOT)
        c0, c1 = o // CAST_CHUNK, (o + 128 * L - 1) // CAST_CHUNK
        R.dma("sp", W[:, slot, 0:L], wbf[o:o + 128 * L].rearrange("(p l) -> p l", p=128),
              reads=[("wbf", c) for c in range(c0, c1 + 1)], writes=[("W", slot)])
        return slot

    def norm(n):
        for s in range(NSUB):
            b = nbank()
            for kc in range(8):
                q = nslot("SQ", 4)
                R.op("act", lambda e, kc=kc, q=q, s=s: e.activation(out=SQ[:, q, :], in_=X[:, kc, sl(s)], func=AF.Square),
                     reads=[("X", kc, s)], writes=[("SQ", q)])
                R.op("pe", lambda e, kc=kc, q=q, b=b: e.matmul(PS[:, b, :], ONESB[:, :], SQ[:, q, :], start=(kc == 0), stop=(kc == 7)),
                     reads=[("SQ", q), ("ONESB",)], writes=[("ps", b)])
            R.op("act", lambda e, b=b, s=s: e.activation(out=SD[:, s, :], in_=PS[:, b, :], func=AF.Ln, scale=1.0 / D, bias=EPSC[:, 0:1]),
                 reads=[("ps", b), ("EPSC",)], writes=[("SD", s)])
            R.op("act", lambda e, s=s: e.activation(out=RS[:, s, :], in_=SD[:, s, :], func=AF.Exp, scale=-0.5),
                 reads=[("SD", s)], writes=[("RS", s)])
            for kc in range(8):
                R.op("dve", lambda e, kc=kc, s=s: e.scalar_tensor_tensor(
                    out=A[:, kc, sl(s)], in0=X[:, kc, sl(s)], scalar=SM[:, COL_G + n * 8 + kc:COL_G + n * 8 + kc + 1],
                    in1=RS[:, s, :], op0=ALU.mult, op1=ALU.mult),
                    reads=[("X", kc, s), ("RS", s), ("SM",)], writes=[("A", kc, s)])

    def ffn(i, j):
        for fc in range(NFC):
            w = wload(("gu", i, j, fc), 2048)
            for s in range(NSUB):
                bg, bu = nbank(), nbank()
                for kc in range(8):
                    R.op("pe", lambda e, kc=kc, w=w, s=s, bg=bg: e.matmul(
                        PS[:, bg, :], W[:, w, kc * 128:(kc + 1) * 128], A[:, kc, sl(s)], start=(kc == 0), stop=(kc == 7)),
                        reads=[("W", w), ("A", kc, s)], writes=[("ps", bg)])
                for kc in range(8):
                    R.op("pe", lambda e, kc=kc, w=w, s=s, bu=bu: e.matmul(
                        PS[:, bu, :], W[:, w, 1024 + kc * 128:1024 + (kc + 1) * 128], A[:, kc, sl(s)], start=(kc == 0), stop=(kc == 7)),
                        reads=[("W", w), ("A", kc, s)], writes=[("ps", bu)])
                g = nslot("SG", 3)
                R.op("act", lambda e, bg=bg, g=g: e.activation(out=SG[:, g, :], in_=PS[:, bg, :], func=AF.Silu),
                     reads=[("ps", bg)], writes=[("SG", g)])
                R.op("dve", lambda e, bu=bu, g=g, fc=fc, s=s: e.tensor_tensor(
                    out=H[:, fc, sl(s)], in0=PS[:, bu, :], in1=SG[:, g, :], op=ALU.mult),
                    reads=[("ps", bu), ("SG", g)], writes=[("H", fc, s)])
        for m in range(8):
            w = wload(("dn", i, j, m), 2816)
            for s in range(NSUB):
                b = nbank()
                for fc in range(NFC):
                    R.op("pe", lambda e, fc=fc, w=w, s=s, b=b: e.matmul(
                        PS[:, b, :], W[:, w, fc * 128:(fc + 1) * 128], H[:, fc, sl(s)], start=(fc == 0), stop=(fc == NFC - 1)),
                        reads=[("W", w), ("H", fc, s)], writes=[("ps", b)])
                R.op("dve", lambda e, m=m, s=s, b=b: e.scalar_tensor_tensor(
                    out=X[:, m, sl(s)], in0=PS[:, b, :], scalar=0.5, in1=X[:, m, sl(s)], op0=ALU.mult, op1=ALU.add),
                    reads=[("ps", b), ("X", m, s)], writes=[("X", m, s)])

    def linear8(keyf, src, srckey, evac):
        for m in range(8):
            w = wload(keyf(m), 1024)
            for s in range(NSUB):
                b = nbank()
                for kc in range(8):
                    R.op("pe", lambda e, kc=kc, w=w, s=s, b=b: e.matmul(
                        PS[:, b, :], W[:, w, kc * 128:(kc + 1) * 128], src[:, kc, sl(s)], start=(kc == 0), stop=(kc == 7)),
                        reads=[("W", w), (srckey, kc, s)], writes=[("ps", b)])
                evac(m, s, b)

    def load_x(src, t0):
        R.dma("pool", X[:, :, :], src[:, t0:t0 + T].rearrange("(kc p) t -> p kc t", p=128),
              reads=[("dX", t0)], writes=[("X", kc, s) for kc in range(8) for s in range(NSUB)])

    def store_x(dst, t0, key):
        return R.dma("pool", dst[:, t0:t0 + T].rearrange("(kc p) t -> p kc t", p=128), X[:, :, :],
                     reads=[("X", kc, s) for kc in range(8) for s in range(NSUB)], writes=[(key, t0)])

    def qkv(l, t0, pos0):
        wv = None
        for s in range(NSUB):
            stages = []
            for step in range(10 + 2):
                if step < 10:
                    m = step
                    w = wload(("qk", l, m), 1024)
                    b = nbank()
                    for kc in range(8):
                        R.op("pe", lambda e, kc=kc, w=w, s=s, b=b: e.matmul(
                            PS[:, b, :], W[:, w, kc * 128:(kc + 1) * 128], A[:, kc, sl(s)], start=(kc == 0), stop=(kc == 7)),
                            reads=[("W", w), ("A", kc, s)], writes=[("ps", b)])
                    z = m % 2
                    R.op("act", lambda e, b=b, z=z: e.activation(out=SQ32[:, z, :], in_=PS[:, b, :], func=AF.Square),
                         reads=[("ps", b)], writes=[("SQ32", z)])
                    stages.append({"b": b, "z": z})
                if 1 <= step < 11:
                    m = step - 1
                    st = stages[m]
                    b, z = st["b"], st["z"]
                    b2 = nbank()
                    R.op("pe", lambda e, b2=b2, z=z: e.matmul(PS[:, b2, :], CM[:, 0, :], SQ32[:, z, :], start=True, stop=True),
                         reads=[("SQ32", z), ("CM",)], writes=[("ps", b2)])
                    R.op("act", lambda e, b2=b2, z=z: e.activation(out=SD[:, z, :], in_=PS[:, b2, :], func=AF.Ln, scale=1.0 / 64, bias=EPSC[:, 0:1]),
                         reads=[("ps", b2), ("EPSC",)], writes=[("SD", z)])
                    R.op("act", lambda e, z=z: e.activation(out=RS[:, z, :], in_=SD[:, z, :], func=AF.Exp, scale=-0.5),
                         reads=[("SD", z)], writes=[("RS", z)])
                    gc = (COL_QG8 + l) if m < 8 else (COL_KG + l)
                    R.op("dve", lambda e, b=b, z=z, gc=gc: e.scalar_tensor_tensor(
                        out=QN[:, z, :], in0=PS[:, b, :], scalar=SM[:, gc:gc + 1], in1=RS[:, z, :], op0=ALU.mult, op1=ALU.mult),
                        reads=[("ps", b), ("RS", z), ("SM",), ("SMq",)], writes=[("QN", z)])
                if 2 <= step < 12:
                    m = step - 2
                    z = stages[m]["z"]
                    b3 = nbank()
                    R.op("pe", lambda e, b3=b3, z=z: e.matmul(PS[:, b3, :], CM[:, 1, :], QN[:, z, :], start=True, stop=True),
                         reads=[("QN", z), ("CM",)], writes=[("ps", b3)])
                    R.op("dve", lambda e, z=z, s=s: e.tensor_tensor(out=T1[:, z, :], in0=QN[:, z, :], in1=CS[:, 0, sl(s)], op=ALU.mult),
                         reads=[("QN", z), ("CS",)], writes=[("T1", z)])
                    R.op("dve", lambda e, z=z, s=s, b3=b3: e.tensor_tensor(out=T2[:, z, :], in0=PS[:, b3, :], in1=CS[:, 1, sl(s)], op=ALU.mult),
                         reads=[("ps", b3), ("CS",)], writes=[("T2", z)])
                    R.op("pool", lambda e, z=z, s=s, m=m: e.tensor_tensor(out=QKST[:, s, m, :], in0=T1[:, z, :], in1=T2[:, z, :], op=ALU.add),
                         reads=[("T1", z), ("T2", z)], writes=[("QKST", s, m)])
            wv = wload(("v", l), 2048)
            for blk in range(4):
                b = nbank()
                for kc in range(8):
                    R.op("pe", lambda e, kc=kc, wv=wv, s=s, b=b, blk=blk: e.matmul(
                        PS[:, b, 0:256], A[:, kc, s * 512 + blk * 128:s * 512 + (blk + 1) * 128],
                        W[:, wv, kc * 256:(kc + 1) * 256], start=(kc == 0), stop=(kc == 7)),
                        reads=[("W", wv), ("A", kc, s)], writes=[("ps", b)])
                R.op("act", lambda e, s=s, b=b, blk=blk: e.activation(
                    out=VST[:, s * 4 + blk, :, 0:64], in_=PS[:, b, 0:256].rearrange("p (g d) -> p g d", g=4), func=AF.Copy),
                    reads=[("ps", b)], writes=[("VST", s, blk)])
            ts = t0 + s * 512
            d1 = R.dma("pool", qs[:, ts:ts + 512].rearrange("(m p) t -> p m t", p=128), QKST[:, s, 0:8, :],
                       reads=[("QKST", s, m) for m in range(8)], writes=[("qs", ts)])
            d2 = R.dma("pool", ks[:, ts:ts + 512].rearrange("(m p) t -> p m t", p=128), QKST[:, s, 8:10, :],
                       reads=[("QKST", s, m) for m in (8, 9)], writes=[("ks", ts)])
            R.dma("pool", vs[ts:ts + 512, :].rearrange("(b p) f -> p b f", p=128),
                  VSTF[:, s * 2048:(s + 1) * 2048].rearrange("p (b f) -> p b f", b=4),
                  reads=[("VST", s, blk) for blk in range(4)], writes=[("vs", ts)])
            R.add_reader(HKEYS, d1)
            R.add_reader(HKEYS, d2)

    def wo(l, t0):
        R.dma("pool", Bz[:, :, :], os_[:, t0:t0 + T].rearrange("(kc p) t -> p kc t", p=128),
              reads=[("os", t0)], writes=[("B", kc, s) for kc in range(8) for s in range(NSUB)])

        def ev(m, s, b):
            R.op("dve", lambda e: e.tensor_tensor(out=X[:, m, sl(s)], in0=PS[:, b, :], in1=X[:, m, sl(s)], op=ALU.add),
                 reads=[("ps", b), ("X", m, s)], writes=[("X", m, s)])
        linear8(lambda m: ("wo", l, m), Bz, "B", ev)

    def pool_in(l, t0):
        def ev(m, s, b):
            R.op("act", lambda e: e.activation(out=U[:, m, sl(s)], in_=PS[:, b, :], func=AF.Copy),
                 reads=[("ps", b)], writes=[("U", m, s)])
        linear8(lambda m: ("win", l, m), A, "A", ev)
        d = R.dma("pool", us[:, t0:t0 + T].rearrange("(kc p) t -> p kc t", p=128), U[:, :, :],
                  reads=[("U", m, s) for m in range(8) for s in range(NSUB)], writes=[("us", t0)])
        R.add_reader(HKEYS, d)

    BKEYS = [("B", kc, s) for kc in range(8) for s in range(NSUB)]

    def load_uh(t0, s0, L):
        first = (t0 == s0)
        last = (t0 + T == s0 + L)
        lo = t0 - (0 if first else 8)
        hi = t0 + T + (0 if last else 8)
        c0 = 8 - (t0 - lo)
        R.dma("pool", UH[:, :, c0:c0 + (hi - lo)], us[:, lo:hi].rearrange("(kc p) t -> p kc t", p=128),
              reads=[("us", "all")], writes=[("UH",)])
        if first:
            R.op("dve", lambda e: e.memset(UH[:, :, 0:8], 0.0), writes=[("UHl",)])
        if last:
            R.op("dve", lambda e: e.memset(UH[:, :, T + 8:T + 16], 0.0), writes=[("UHr",)])

    def pool_mix(l, t0, s0, L):
        first = (t0 == s0)
        last = (t0 + T == s0 + L)
        NCOL = T + 16
        for g, w in enumerate(POOLW):
            half = w // 2
            ch = slice(2 * g, 2 * g + 2)
            src = UH
            bufs = [TA, TB]
            n = NCOL
            k = 1
            cur = None
            idx = 0
            while k < w:
                n = n - k
                dst = bufs[idx % 2]
                if cur is None:
                    R.op("dve", lambda e, dst=dst, n=n, k=k, ch=ch: e.tensor_tensor(
                        out=dst[:, :, 0:n], in0=UH[:, ch, 0:n], in1=UH[:, ch, k:k + n], op=ALU.add),
                        reads=[("UH",), ("UHl",), ("UHr",)], writes=[("TT", idx % 2)] + BKEYS)
                else:
                    R.op("dve", lambda e, dst=dst, cur=cur, n=n, k=k: e.tensor_tensor(
                        out=dst[:, :, 0:n], in0=cur[:, :, 0:n], in1=cur[:, :, k:k + n], op=ALU.add),
                        reads=[("TT", (idx - 1) % 2)], writes=[("TT", idx % 2)] + BKEYS)
                cur = dst
                idx += 1
                k *= 2
            sb = 8 - half
            tk = ("TT", (idx - 1) % 2)
            R.op("dve", lambda e, cur=cur, sb=sb, w=w, ch=ch: e.scalar_tensor_tensor(
                out=A[:, ch, :], in0=cur[:, :, sb:sb + T], scalar=1.0 / w, in1=UH[:, ch, 8:8 + T], op0=ALU.mult, op1=ALU.subtract),
                reads=[tk, ("UH",)], writes=[("A", kc, s) for kc in (2 * g, 2 * g + 1) for s in range(NSUB)])
            fix = []
            if first:
                fix += [(t, t + half) for t in range(half)]
            if last:
                fix += [(t, (T - t) + half) for t in range(T - half + 1, T)]
            for (t, cnt) in fix:
                R.op("dve", lambda e, cur=cur, sb=sb, t=t, cnt=cnt, ch=ch: e.scalar_tensor_tensor(
                    out=A[:, ch, t:t + 1], in0=cur[:, :, sb + t:sb + t + 1], scalar=1.0 / cnt, in1=UH[:, ch, 8 + t:9 + t],
                    op0=ALU.mult, op1=ALU.subtract),
                    reads=[tk, ("UH",)], writes=[("A", kc, t // 512) for kc in (2 * g, 2 * g + 1)])
        wg = wload(("wg", l), 2048)
        for g in range(4):
            for m2 in range(2):
                for s in range(NSUB):
                    b = nbank()
                    for k2 in range(2):
                        c = ((g * 2 + k2) * 2 + m2) * 128
                        R.op("pe", lambda e, c=c, wg=wg, g=g, k2=k2, s=s, b=b: e.matmul(
                            PS[:, b, :], W[:, wg, c:c + 128], A[:, 2 * g + k2, sl(s)], start=(k2 == 0), stop=(k2 == 1)),
                            reads=[("W", wg), ("A", 2 * g + k2, s)], writes=[("ps", b)])
                    R.op("act", lambda e, g=g, m2=m2, s=s, b=b: e.activation(out=Bz[:, 2 * g + m2, sl(s)], in_=PS[:, b, :], func=AF.Copy),
                         reads=[("ps", b)], writes=[("B", 2 * g + m2, s)] + ([("TT", 0), ("TT", 1)] if (g, m2, s) == (0, 0, 0) else []))

        def ev(m, s, b):
            R.op("dve", lambda e: e.scalar_tensor_tensor(
                out=X[:, m, sl(s)], in0=PS[:, b, :], scalar=SM[:, COL_PS + l * 8 + m:COL_PS + l * 8 + m + 1], in1=X[:, m, sl(s)],
                op0=ALU.mult, op1=ALU.add),
                reads=[("ps", b), ("X", m, s), ("SM",)], writes=[("X", m, s)])
        linear8(lambda m: ("wout", l, m), Bz, "B", ev)

    def attention():
        R.op("dve", lambda e: e.memset(QZ[:, :, :, :], 0.0), writes=[("QZ", 0), ("QZ", 1)])
        s0 = 0
        it = 0
        for L in seqs:
            nkt = L // 128
            for g in range(4):
                for hb in (0, 64):
                    R.dma("sp", KD[hb:hb + 64, g, 0:L], ks[g * 64:(g + 1) * 64, s0:s0 + L], reads=[("ks", "all")], writes=[("KD", g)])
            step = 16
            for k0 in range(0, nkt, step):
                k1 = min(nkt, k0 + step)
                R.dma("sp", VA[:, k0:k1, :, :].rearrange("p k g d -> p k (g d)"),
                      vs[s0 + k0 * 128:s0 + k1 * 128, :].rearrange("(k p) f -> p k f", p=128),
                      reads=[("vs", "all")], writes=[("VA",)])
            iters = [(g, hp, qt) for g in range(4) for hp in range(2) for qt in range(L // 512)]

            def load_q(k, s0=s0):
                g, hp, qt = iters[k]
                c = 2 * g + hp
                tq = s0 + qt * 512
                for r in range(2):
                    R.dma("sp", QZ[r * 64:(r + 1) * 64, (it0 + k) % 2, r, :], qs[c * 128 + r * 64:c * 128 + (r + 1) * 64, tq:tq + 512],
                          reads=[("qs", "all")], writes=[("QZ", (it0 + k) % 2)])

            it0 = it
            load_q(0)
            for k, (g, hp, qt) in enumerate(iters):
                c = 2 * g + hp
                tq = s0 + qt * 512
                qsl = it % 2
                it += 1
                if k + 1 < len(iters):
                    load_q(k + 1)
                LA = 2

                def QK(i, g=g, qsl=qsl):
                    j = i % 3
                    for r in range(2):
                        sb = 2 + 2 * j + r
                        R.op("pe", lambda e, r=r, sb=sb: e.matmul(PS[:, sb, :], KD[:, g, i * 128:(i + 1) * 128],
                                                                 QZ[:, qsl, r, :], start=True, stop=True),
                             reads=[("KD", g), ("QZ", qsl)], writes=[("ps", sb)])
                    R.op("act", lambda e: e.activation(out=PT[:, j, :], in_=PS[:, 2 + 2 * j:4 + 2 * j, :].rearrange("p a b -> p (a b)"), func=AF.Exp),
                         reads=[("ps", 2 + 2 * j), ("ps", 3 + 2 * j)], writes=[("PT", j)])

                def PV(i, g=g, nkt=nkt):
                    j = i % 3
                    for r in range(2):
                        R.op("pe", lambda e, r=r: e.matmul(PS[:, r, :], VA[:, i, g, :], PT[:, j, r * 512:(r + 1) * 512],
                                                           start=(i == 0), stop=(i == nkt - 1)),
                             reads=[("VA",), ("PT", j)], writes=[("ps", r)])

                for i in range(nkt + LA):
                    if i < nkt:
                        QK(i)
                    if i >= LA:
                        PV(i - LA)
                for r in range(2):
                    R.op("dve", lambda e, r=r, qsl=qsl: e.tensor_copy(out=OC[0:64, qsl, r, :], in_=PS[0:64, r, :]),
                         reads=[("ps", r)], writes=[("OC", qsl, r)])
                    R.op("dve", lambda e, r=r, qsl=qsl: e.tensor_copy(out=DC[0:64, qsl, r, :], in_=PS[64:128, r, :]),
                         reads=[("ps", r)], writes=[("DC", qsl, r)])
                for r in range(2):
                    R.op("dve", lambda e, r=r, qsl=qsl: e.reciprocal(out=REC[0:64, r, :], in_=DC[0:64, qsl, r, :]),
                         reads=[("DC", qsl, r)], writes=[("REC", r)])
                    R.op("dve", lambda e, r=r, qsl=qsl: e.tensor_tensor(
                        out=OST[r * 64:(r + 1) * 64, qsl, :], in0=OC[0:64, qsl, r, :], in1=REC[0:64, r, :], op=ALU.mult),
                        reads=[("OC", qsl, r), ("REC", r)], writes=[("OST", qsl)])
                R.dma("pool", os_[c * 128:(c + 1) * 128, tq:tq + 512], OST[:, qsl, :],
                      reads=[("OST", qsl)], writes=[("os", tq)])
            s0 += L

    def phase_ok(p):
        return phases is None or p in phases

    def vst_ones():
        R.op("dve", lambda e: e.memset(VSTF[:, :], 1.0), writes=[("VST", s, blk) for s in range(NSUB) for blk in range(4)])

    vst_ones()
    def nxt(k):
        return tiles[k + 1][0] if k + 1 < len(tiles) else None

    def load_cs(pos0):
        R.dma("pool", CS[:, :, :], rope[:, :, pos0:pos0 + T].rearrange("a p t -> p a t"), writes=[("CS",)])

    if phase_ok(1):
        load_x(xT, tiles[0][0])
        for k, (t0, s0, L) in enumerate(tiles):
            load_cs(t0 - s0)
            norm(0); ffn(0, 0)
            norm(1)
            store_x(xs, t0, "xs")
            if nxt(k) is not None:
                load_x(xT, nxt(k))
            qkv(0, t0, t0 - s0)
        R.barrier()
    if phase_ok(2):
        attention()
        R.barrier()
    if phase_ok(3):
        load_x(xs, tiles[0][0])
        for k, (t0, s0, L) in enumerate(tiles):
            wo(0, t0)
            norm(2); ffn(0, 1)
            norm(3); ffn(1, 0)
            norm(4)
            store_x(xs, t0, "xs")
            if nxt(k) is not None:
                load_x(xs, nxt(k))
            pool_in(0, t0)
        R.barrier()
    if phase_ok(4):
        load_x(xs, tiles[0][0])
        load_uh(*tiles[0])
        for k, (t0, s0, L) in enumerate(tiles):
            load_cs(t0 - s0)
            pool_mix(0, t0, s0, L)
            if k + 1 < len(tiles):
                load_uh(*tiles[k + 1])
            norm(5); ffn(1, 1)
            norm(6); ffn(2, 0)
            norm(7)
            store_x(xs, t0, "xs")
            if nxt(k) is not None:
                load_x(xs, nxt(k))
            qkv(1, t0, t0 - s0)
        R.barrier()
    if phase_ok(5):
        attention()
        R.barrier()
    if phase_ok(6):
        load_x(xs, tiles[0][0])
        for k, (t0, s0, L) in enumerate(tiles):
            wo(1, t0)
            norm(8); ffn(2, 1)
            norm(9); ffn(3, 0)
            norm(10)
            store_x(xs, t0, "xs")
            if nxt(k) is not None:
                load_x(xs, nxt(k))
            pool_in(1, t0)
        R.barrier()
    if phase_ok(7):
        load_uh(*tiles[0])
        for k, (t0, s0, L) in enumerate(tiles):
            load_x(xs, t0)
            pool_mix(1, t0, s0, L)
            if k + 1 < len(tiles):
                load_uh(*tiles[k + 1])
            norm(11); ffn(3, 1)
            store_x(yT, t0, "y")
        R.barrier()
    elif phases is not None:
        for (t0, s0, L) in tiles:
            load_x(xs, t0)
            store_x(yT, t0, "y")
        R.barrier()

    nsem = R.finalize()
    from contextlib import ExitStack
    with ExitStack() as es:
        sems = {}
        for e in Rec.ENGS:
            for ep in range(nsem[e]):
                sems[(e, ep)] = es.enter_context(nc.semaphore(f"s_{e}_{ep}"))
        for i in range(NDMASEM):
            sems[("dma", i)] = es.enter_context(nc.semaphore(f"s_dma_{i}"))
        block = es.enter_context(nc.Block())

        def replay(name, eng):
            waited = {}
            for o in R.streams[name]:
                for d in o.deps:
                    if waited.get(d.sem, 0) >= d.val:
                        continue
                    eng.wait_ge(sems[d.sem], d.val)
                    waited[d.sem] = d.val
                ins = o.fn(eng)
                if o.sig:
                    ins.then_inc(sems[o.sem], 16 if o.is_dma else 1)
            for d in R.pending[name]:
                if d.sem is None or waited.get(d.sem, 0) >= d.val:
                    continue
                eng.wait_ge(sems[d.sem], d.val)
                waited[d.sem] = d.val

        @block.sync
        def _(e):
            replay("sp", e)

        @block.scalar
        def _(e):
            replay("act", e)

        @block.vector
        def _(e):
            replay("dve", e)

        @block.gpsimd
        def _(e):
            replay("pool", e)

        @block.tensor
        def _(e):
            replay("pe", e)
    return nc


def make_in_maps(inputs, seqs, xts):
    wall = pack_weights(inputs)
    sm = pack_small(inputs)
    cm = const_mats()
    rp = rope_tables()
    return [{"xT": x, "wall": wall, "small": sm, "cmat": cm, "rope": rp} for x in xts]


def kernel(**inputs):
    inputs = {k: np.asarray(v) for k, v in inputs.items()}
    xp = inputs["x_prompt"]
    xsm = inputs["x_sample"]
    n = 8
    xts = []
    for c in range(n):
        rows = np.concatenate([xp[2 * c], xp[2 * c + 1], xsm[c]], axis=0)
        xts.append(np.ascontiguousarray(rows.T))
    nc = build_program(SEQS_FULL)
    in_maps = make_in_maps(inputs, SEQS_FULL, xts)
    res = run_bass_kernel_spmd(nc, in_maps, core_ids=list(range(n)))
    yp = np.empty_like(xp)
    ysm = np.empty_like(xsm)
    for c in range(n):
        y = np.asarray(res.results[c]["yT"]).T
        yp[2 * c] = y[0:4096]
        yp[2 * c + 1] = y[4096:8192]
        ysm[c] = y[8192:16384]
    return (yp, ysm)
```

```python
import numpy as np
import concourse.bass as bass
import concourse.mybir as mybir
from concourse.bass_utils import run_bass_kernel_spmd

F32 = mybir.dt.float32
BF16 = mybir.dt.bfloat16
ALU = mybir.AluOpType
AF = mybir.ActivationFunctionType

D = 1024
DFF = 2816
NFC = 22
NKC = 8
T = 1024
NSUB = T // 512
EPS = 1e-6
DEPTH = 4
POOLW = (2, 4, 8, 16)
SMAX = 8192
SEQS_FULL = (4096, 4096, 8192)
EPOCH = 16000
NDMASEM = 24
CAST_CHUNK = 1 << 21


def weight_layout():
    off = {}
    cur = 0

    def add(key, n):
        nonlocal cur
        off[key] = cur
        cur += n

    def ffn(i, j):
        for fc in range(NFC):
            add(("gu", i, j, fc), 128 * 2048)
        for m in range(8):
            add(("dn", i, j, m), 128 * 2816)

    def attn_qkv(l):
        for m in range(10):
            add(("qk", l, m), 128 * 1024)
        add(("v", l), 128 * 2048)

    def attn_o(l):
        for m in range(8):
            add(("wo", l, m), 128 * 1024)

    def pool_in(l):
        for m in range(8):
            add(("win", l, m), 128 * 1024)

    def pool_rest(l):
        add(("wg", l), 128 * 2048)
        for m in range(8):
            add(("wout", l, m), 128 * 1024)

    ffn(0, 0); attn_qkv(0); attn_o(0); ffn(0, 1)
    ffn(1, 0); pool_in(0); pool_rest(0); ffn(1, 1)
    ffn(2, 0); attn_qkv(1); attn_o(1); ffn(2, 1)
    ffn(3, 0); pool_in(1); pool_rest(1); ffn(3, 1)
    return off, cur


def pack_weights(inp):
    off, total = weight_layout()
    wall = np.empty(total, np.float32)

    def put(key, arr):
        a = np.ascontiguousarray(arr, dtype=np.float32).reshape(-1)
        wall[off[key]:off[key] + a.size] = a

    for i in range(DEPTH):
        for j in range(2):
            wg = inp["ffn_w_gate"][i, j].reshape(8, 128, NFC, 128)
            wu = inp["ffn_w_up"][i, j].reshape(8, 128, NFC, 128)
            gu = np.stack([wg, wu], 0).transpose(3, 2, 0, 1, 4)
            for fc in range(NFC):
                put(("gu", i, j, fc), gu[fc])
            wd = inp["ffn_w_down"][i, j].reshape(NFC, 128, 8, 128).transpose(2, 1, 0, 3)
            for m in range(8):
                put(("dn", i, j, m), wd[m])
    for l in range(2):
        wq = inp["attn_w_qkv"][l]
        qk = wq[:, :1280].reshape(8, 128, 10, 128).transpose(2, 1, 0, 3)
        for m in range(10):
            put(("qk", l, m), qk[m])
        put(("v", l), wq[:, 1280:].reshape(8, 128, 256).transpose(1, 0, 2))
        wo = inp["attn_w_o"][l].reshape(8, 128, 8, 128).transpose(2, 1, 0, 3)
        for m in range(8):
            put(("wo", l, m), wo[m])
        wi = inp["pool_w_in"][l].reshape(8, 128, 8, 128).transpose(2, 1, 0, 3)
        wt = inp["pool_w_out"][l].reshape(8, 128, 8, 128).transpose(2, 1, 0, 3)
        for m in range(8):
            put(("win", l, m), wi[m])
            put(("wout", l, m), wt[m])
        put(("wg", l), inp["pool_w_group"][l].reshape(4, 2, 128, 2, 128).transpose(2, 0, 1, 3, 4))
    return wall


COL_G = 0
COL_PS = 96
COL_QG = 112
COL_KG = 114
COL_QG8 = 116
NS = 120


def pack_small(inp):
    sm = np.zeros((128, NS), np.float32)
    g = inp["norm_gains"].reshape(12, 8, 128)
    sm[:, COL_G:COL_G + 96] = g.transpose(2, 0, 1).reshape(128, 96)
    ps = inp["pool_scale"].reshape(2, 8, 128)
    sm[:, COL_PS:COL_PS + 16] = ps.transpose(2, 0, 1).reshape(128, 16)
    for l in range(2):
        sm[:, COL_QG + l] = np.tile(inp["attn_q_gain"][l], 2)
        sm[:, COL_KG + l] = np.tile(inp["attn_k_gain"][l], 2)
    return sm


def const_mats():
    cm = np.zeros((2, 128, 128), np.float32)
    cm[0, :64, :64] = 1.0
    cm[0, 64:, 64:] = 1.0
    for m in range(128):
        cm[1, m ^ 1, m] = 1.0
    return cm


def rope_tables():
    t = np.arange(SMAX)
    row = (t // 64).astype(np.float32)
    col = (t % 64).astype(np.float32)
    freqs = (np.float32(10000.0) ** (-np.arange(16, dtype=np.float32) / np.float32(16))).astype(np.float32)
    ang = np.concatenate([row[:, None] * freqs, col[:, None] * freqs], -1).astype(np.float32)
    cos = np.cos(ang).astype(np.float32)
    sin = np.sin(ang).astype(np.float32)
    tab = np.zeros((2, 128, SMAX), np.float32)
    for p in range(128):
        d = p % 64
        i = d // 2
        tab[0, p] = cos[:, i]
        tab[1, p] = -sin[:, i] if d % 2 == 0 else sin[:, i]
    return tab


class Op:
    __slots__ = ("eng", "fn", "deps", "sig", "is_dma", "sem", "val", "idx")

    def __init__(self, eng, fn, is_dma=False):
        self.eng = eng
        self.fn = fn
        self.deps = []
        self.sig = False
        self.is_dma = is_dma
        self.sem = None
        self.val = 0


class Rec:
    ENGS = ("sp", "act", "dve", "pool", "pe")

    def __init__(self):
        self.streams = {e: [] for e in self.ENGS}
        self.track = {}
        self.last_dma_on_sem = [None] * NDMASEM
        self.dma_cnt = [0] * NDMASEM
        self.dma_rr = 0
        self.dma_rrq = [0, 0]
        self.pending = {e: [] for e in self.ENGS}

    def _dep(self, op, d, war):
        if d is None or d is op:
            return
        if not d.is_dma and not op.is_dma and d.eng == op.eng:
            if op.eng == "pe":
                return
            if war:
                return
        d.sig = True
        op.deps.append(d)

    def op(self, eng, fn, reads=(), writes=(), is_dma=False, extra=()):
        o = Op(eng, fn, is_dma)
        for d in self.pending[eng]:
            self._dep(o, d, False)
        self.pending[eng] = []
        for d in extra:
            self._dep(o, d, False)
        for k in reads:
            st = self.track.get(k)
            if st is not None:
                self._dep(o, st[0], False)
        for k in writes:
            st = self.track.get(k)
            if st is not None:
                self._dep(o, st[0], True)
                for r in st[1]:
                    self._dep(o, r, True)
        for k in reads:
            st = self.track.get(k)
            if st is None:
                self.track[k] = [None, [o]]
            else:
                st[1].append(o)
        for k in writes:
            self.track[k] = [o, []]
        if is_dma:
            half = NDMASEM // 2
            qi = 0 if eng == "sp" else 1
            s = qi * half + self.dma_rrq[qi]
            self.dma_rrq[qi] = (self.dma_rrq[qi] + 1) % half
            prev = self.last_dma_on_sem[s]
            if prev is not None:
                o.deps.append(prev)
            self.dma_cnt[s] += 1
            o.sem = ("dma", s)
            o.val = 16 * self.dma_cnt[s]
            o.sig = True
            self.last_dma_on_sem[s] = o
        self.streams[eng].append(o)
        return o

    def dma(self, eng, out, in_, reads=(), writes=()):
        return self.op(eng, lambda e: e.dma_start(out=out, in_=in_), reads, writes, is_dma=True)

    def all_deps(self):
        deps = []
        for e in self.ENGS:
            for o in reversed(self.streams[e]):
                if not o.is_dma:
                    deps.append(o)
                    break
        for o in self.last_dma_on_sem:
            if o is not None:
                deps.append(o)
        return deps

    def add_reader(self, keys, o):
        for k in keys:
            st = self.track.get(k)
            if st is None:
                self.track[k] = [None, [o]]
            else:
                st[1].append(o)

    def barrier(self):
        deps = []
        for e in self.ENGS:
            for o in reversed(self.streams[e]):
                if not o.is_dma:
                    deps.append(o)
                    break
        for o in self.last_dma_on_sem:
            if o is not None:
                deps.append(o)
        for e in self.ENGS:
            self.pending[e] = list(deps)
        self.track = {}

    def finalize(self):
        nsem = {}
        for e in self.ENGS:
            ep = 0
            cnt = 0
            for o in self.streams[e]:
                if o.is_dma or not o.sig:
                    continue
                if cnt >= EPOCH:
                    ep += 1
                    cnt = 0
                cnt += 1
                o.sem = (e, ep)
                o.val = cnt
            nsem[e] = ep + 1
        return nsem


def tile_info(seqs):
    out = []
    s0 = 0
    for L in seqs:
        for a in range(0, L, T):
            out.append((s0 + a, s0, L))
        s0 += L
    return out


def build_program(seqs=SEQS_FULL, phases=None):
    NTOK = sum(seqs)
    tiles = tile_info(seqs)
    woff, NW = weight_layout()
    nc = bass.Bass("TRN2", target_bir_lowering=False)

    xT = nc.dram_tensor("xT", [D, NTOK], F32, kind="ExternalInput").ap()
    wall = nc.dram_tensor("wall", [NW], F32, kind="ExternalInput").ap()
    small = nc.dram_tensor("small", [128, NS], F32, kind="ExternalInput").ap()
    cmat = nc.dram_tensor("cmat", [2, 128, 128], F32, kind="ExternalInput").ap()
    rope = nc.dram_tensor("rope", [2, 128, SMAX], F32, kind="ExternalInput").ap()
    yT = nc.dram_tensor("yT", [D, NTOK], F32, kind="ExternalOutput").ap()
    wbf = nc.dram_tensor("wbf", [NW], BF16).ap()
    xs = nc.dram_tensor("xs", [D, NTOK], F32).ap()
    us = nc.dram_tensor("us", [D, NTOK], F32).ap()
    qs = nc.dram_tensor("qs", [D, NTOK], BF16).ap()
    ks = nc.dram_tensor("ks", [256, NTOK], BF16).ap()
    vs = nc.dram_tensor("vs", [NTOK, 512], BF16).ap()
    os_ = nc.dram_tensor("os", [D, NTOK], BF16).ap()

    base = [17408]

    def alloc(name, shape, dt, at=None):
        n = int(np.prod(shape[1:])) * (4 if dt == F32 else 2)
        n = (n + 63) // 64 * 64
        if at is None:
            at = base[0]
            base[0] += n
        assert at + n <= 229376 - 512, (name, at, n)
        return nc.alloc_sbuf_tensor_at(name, list(shape), dt, offset=at)

    SM = alloc("SM", [128, NS], F32)
    CM = alloc("CM", [128, 2, 128], F32)
    ONESB = alloc("ONESB", [128, 128], BF16)
    EPSC = alloc("EPSC", [128, 16], F32)
    vst_at = base[0]
    VST = alloc("VST", [128, 8, 4, 128], BF16)
    VSTF = alloc("VSTF", [128, 4096], BF16, at=vst_at)
    common_end = base[0]
    X = alloc("X", [128, 8, T], F32)
    A = alloc("A", [128, 8, T], BF16)
    B = alloc("B", [128, 2, T + 16, 2], F32)
    H = alloc("H", [128, NFC, T], BF16)
    NW_SLOT = 5
    W = alloc("W", [128, NW_SLOT, 2816], BF16)
    CS = alloc("CS", [128, 2, T], F32)
    SQ = alloc("SQ", [128, 4, 512], BF16)
    SD = alloc("SD", [128, 2, 512], F32)
    RS = alloc("RS", [128, 2, 512], F32)
    SG = alloc("SG", [128, 3, 512], F32)
    UH = alloc("UH", [128, 8, T + 16], F32)
    tl_end = base[0]
    offs = {}
    cur = common_end
    for nm, nbytes in (("X", 8 * T * 4), ("A", 8 * T * 2), ("B", 2 * (T + 16) * 2 * 4), ("H", NFC * T * 2)):
        offs[nm] = cur
        cur += (nbytes + 63) // 64 * 64
    Bz = alloc("Bz", [128, 8, T], BF16, at=offs["B"])
    TA = alloc("TA", [128, 2, T + 16], F32, at=offs["B"])
    TB = alloc("TB", [128, 2, T + 16], F32, at=offs["B"] + 2 * (T + 16) * 4)
    U = alloc("U", [128, 8, T], F32, at=offs["H"])
    hq = offs["H"]
    QKST = alloc("QKST", [128, 2, 10, 512], BF16, at=hq); hq += 2 * 10 * 512 * 2
    SQ32 = alloc("SQ32", [128, 2, 512], F32, at=hq); hq += 2 * 512 * 4
    QN = alloc("QN", [128, 2, 512], F32, at=hq); hq += 2 * 512 * 4
    T1 = alloc("T1", [128, 2, 512], F32, at=hq); hq += 2 * 512 * 4
    T2 = alloc("T2", [128, 2, 512], F32, at=hq); hq += 2 * 512 * 4
    assert hq <= offs["H"] + NFC * T * 2
    cur = common_end
    smx = max(seqs)
    KD = alloc("KD", [128, 4, smx], BF16, at=cur); cur += 4 * smx * 2
    VA = alloc("VA", [128, smx // 128, 4, 128], BF16, at=cur); cur += (smx // 128) * 512 * 2
    QZ = alloc("QZ", [128, 2, 2, 512], BF16, at=cur); cur += 2 * 2 * 512 * 2
    PT = alloc("PT", [128, 3, 1024], BF16, at=cur); cur += 3 * 1024 * 2
    OST = alloc("OST", [128, 2, 512], BF16, at=cur); cur += 2 * 512 * 2
    REC = alloc("REC", [128, 2, 512], F32, at=cur); cur += 2 * 512 * 4
    OC = alloc("OC", [128, 2, 2, 512], F32, at=cur); cur += 2 * 2 * 512 * 4
    DC = alloc("DC", [128, 2, 2, 512], F32, at=cur); cur += 2 * 2 * 512 * 4
    assert cur <= 229376 - 512, cur

    PS = nc.alloc_psum_tensor("PS", [128, 8, 512], F32)

    R = Rec()
    bank_rr = [0]

    def nbank():
        b = bank_rr[0]
        bank_rr[0] = (b + 1) % 8
        return b

    ring = {"W": 0, "SQ": 0, "SG": 0}

    def nslot(name, n):
        s = ring[name]
        ring[name] = (s + 1) % n
        return s

    def sl(s):
        return slice(s * 512, (s + 1) * 512)

    HKEYS = [("H", fc, s) for fc in range(NFC) for s in range(NSUB)]

    R.dma("sp", SM[:, :], small[:, :], writes=[("SM",)])
    R.dma("sp", CM[:, :, :], cmat.rearrange("a p c -> p a c"), writes=[("CM",)])
    R.op("dve", lambda e: e.memset(ONESB[:, :], 1.0), writes=[("ONESB",)])
    R.op("dve", lambda e: e.memset(EPSC[:, :], EPS), writes=[("EPSC",)])
    R.op("dve", lambda e: e.tensor_scalar(SM[:, COL_QG8:COL_QG8 + 2], SM[:, COL_QG:COL_QG + 2], 0.125, None, op0=ALU.mult),
         reads=[("SM",)], writes=[("SMq",)])
    nchunk = (NW + CAST_CHUNK - 1) // CAST_CHUNK
    for c in range(nchunk):
        a, b = c * CAST_CHUNK, min(NW, (c + 1) * CAST_CHUNK)
        R.dma("pool", wbf[a:b].rearrange("(r c) -> r c", c=2048), wall[a:b].rearrange("(r c) -> r c", c=2048),
              writes=[("wbf", c)])

    def wload(key, L):
        o = woff[key]
        slot = nslot("W", NW_SLOT)
        c0, c1 = o // CAST_CHUNK, (o + 128 * L - 1) // CAST_CHUNK
        R.dma("sp", W[:, slot, 0:L], wbf[o:o + 128 * L].rearrange("(p l) -> p l", p=128),
              reads=[("wbf", c) for c in range(c0, c1 + 1)], writes=[("W", slot)])
        return slot

    def norm(n):
        for s in range(NSUB):
            b = nbank()
            for kc in range(8):
                q = nslot("SQ", 4)
                R.op("act", lambda e, kc=kc, q=q, s=s: e.activation(out=SQ[:, q, :], in_=X[:, kc, sl(s)], func=AF.Square),
                     reads=[("X", kc, s)], writes=[("SQ", q)])
                R.op("pe", lambda e, kc=kc, q=q, b=b: e.matmul(PS[:, b, :], ONESB[:, :], SQ[:, q, :], start=(kc == 0), stop=(kc == 7)),
                     reads=[("SQ", q), ("ONESB",)], writes=[("ps", b)])
            R.op("act", lambda e, b=b, s=s: e.activation(out=SD[:, s, :], in_=PS[:, b, :], func=AF.Ln, scale=1.0 / D, bias=EPSC[:, 0:1]),
                 reads=[("ps", b), ("EPSC",)], writes=[("SD", s)])
            R.op("act", lambda e, s=s: e.activation(out=RS[:, s, :], in_=SD[:, s, :], func=AF.Exp, scale=-0.5),
                 reads=[("SD", s)], writes=[("RS", s)])
            for kc in range(8):
                R.op("dve", lambda e, kc=kc, s=s: e.scalar_tensor_tensor(
                    out=A[:, kc, sl(s)], in0=X[:, kc, sl(s)], scalar=SM[:, COL_G + n * 8 + kc:COL_G + n * 8 + kc + 1],
                    in1=RS[:, s, :], op0=ALU.mult, op1=ALU.mult),
                    reads=[("X", kc, s), ("RS", s), ("SM",)], writes=[("A", kc, s)])

    def ffn(i, j):
        for fc in range(NFC):
            w = wload(("gu", i, j, fc), 2048)
            for s in range(NSUB):
                bg, bu = nbank(), nbank()
                for kc in range(8):
                    R.op("pe", lambda e, kc=kc, w=w, s=s, bg=bg: e.matmul(
                        PS[:, bg, :], W[:, w, kc * 128:(kc + 1) * 128], A[:, kc, sl(s)], start=(kc == 0), stop=(kc == 7)),
                        reads=[("W", w), ("A", kc, s)], writes=[("ps", bg)])
                for kc in range(8):
                    R.op("pe", lambda e, kc=kc, w=w, s=s, bu=bu: e.matmul(
                        PS[:, bu, :], W[:, w, 1024 + kc * 128:1024 + (kc + 1) * 128], A[:, kc, sl(s)], start=(kc == 0), stop=(kc == 7)),
                        reads=[("W", w), ("A", kc, s)], writes=[("ps", bu)])
                g = nslot("SG", 3)
                R.op("act", lambda e, bg=bg, g=g: e.activation(out=SG[:, g, :], in_=PS[:, bg, :], func=AF.Silu),
                     reads=[("ps", bg)], writes=[("SG", g)])
                R.op("dve", lambda e, bu=bu, g=g, fc=fc, s=s: e.tensor_tensor(
                    out=H[:, fc, sl(s)], in0=PS[:, bu, :], in1=SG[:, g, :], op=ALU.mult),
                    reads=[("ps", bu), ("SG", g)], writes=[("H", fc, s)])
        for m in range(8):
            w = wload(("dn", i, j, m), 2816)
            for s in range(NSUB):
                b = nbank()
                for fc in range(NFC):
                    R.op("pe", lambda e, fc=fc, w=w, s=s, b=b: e.matmul(
                        PS[:, b, :], W[:, w, fc * 128:(fc + 1) * 128], H[:, fc, sl(s)], start=(fc == 0), stop=(fc == NFC - 1)),
                        reads=[("W", w), ("H", fc, s)], writes=[("ps", b)])
                R.op("dve", lambda e, m=m, s=s, b=b: e.scalar_tensor_tensor(
                    out=X[:, m, sl(s)], in0=PS[:, b, :], scalar=0.5, in1=X[:, m, sl(s)], op0=ALU.mult, op1=ALU.add),
                    reads=[("ps", b), ("X", m, s)], writes=[("X", m, s)])

    def linear8(keyf, src, srckey, evac):
        for m in range(8):
            w = wload(keyf(m), 1024)
            for s in range(NSUB):
                b = nbank()
                for kc in range(8):
                    R.op("pe", lambda e, kc=kc, w=w, s=s, b=b: e.matmul(
                        PS[:, b, :], W[:, w, kc * 128:(kc + 1) * 128], src[:, kc, sl(s)], start=(kc == 0), stop=(kc == 7)),
                        reads=[("W", w), (srckey, kc, s)], writes=[("ps", b)])
                evac(m, s, b)

    def load_x(src, t0):
        R.dma("pool", X[:, :, :], src[:, t0:t0 + T].rearrange("(kc p) t -> p kc t", p=128),
              reads=[("dX", t0)], writes=[("X", kc, s) for kc in range(8) for s in range(NSUB)])

    def store_x(dst, t0, key):
        return R.dma("pool", dst[:, t0:t0 + T].rearrange("(kc p) t -> p kc t", p=128), X[:, :, :],
                     reads=[("X", kc, s) for kc in range(8) for s in range(NSUB)], writes=[(key, t0)])

    def qkv(l, t0, pos0):
        wv = None
        for s in range(NSUB):
            stages = []
            for step in range(10 + 2):
                if step < 10:
                    m = step
                    w = wload(("qk", l, m), 1024)
                    b = nbank()
                    for kc in range(8):
                        R.op("pe", lambda e, kc=kc, w=w, s=s, b=b: e.matmul(
                            PS[:, b, :], W[:, w, kc * 128:(kc + 1) * 128], A[:, kc, sl(s)], start=(kc == 0), stop=(kc == 7)),
                            reads=[("W", w), ("A", kc, s)], writes=[("ps", b)])
                    z = m % 2
                    R.op("act", lambda e, b=b, z=z: e.activation(out=SQ32[:, z, :], in_=PS[:, b, :], func=AF.Square),
                         reads=[("ps", b)], writes=[("SQ32", z)])
                    stages.append({"b": b, "z": z})
                if 1 <= step < 11:
                    m = step - 1
                    st = stages[m]
                    b, z = st["b"], st["z"]
                    b2 = nbank()
                    R.op("pe", lambda e, b2=b2, z=z: e.matmul(PS[:, b2, :], CM[:, 0, :], SQ32[:, z, :], start=True, stop=True),
                         reads=[("SQ32", z), ("CM",)], writes=[("ps", b2)])
                    R.op("act", lambda e, b2=b2, z=z: e.activation(out=SD[:, z, :], in_=PS[:, b2, :], func=AF.Ln, scale=1.0 / 64, bias=EPSC[:, 0:1]),
                         reads=[("ps", b2), ("EPSC",)], writes=[("SD", z)])
                    R.op("act", lambda e, z=z: e.activation(out=RS[:, z, :], in_=SD[:, z, :], func=AF.Exp, scale=-0.5),
                         reads=[("SD", z)], writes=[("RS", z)])
                    gc = (COL_QG8 + l) if m < 8 else (COL_KG + l)
                    R.op("dve", lambda e, b=b, z=z, gc=gc: e.scalar_tensor_tensor(
                        out=QN[:, z, :], in0=PS[:, b, :], scalar=SM[:, gc:gc + 1], in1=RS[:, z, :], op0=ALU.mult, op1=ALU.mult),
                        reads=[("ps", b), ("RS", z), ("SM",), ("SMq",)], writes=[("QN", z)])
                if 2 <= step < 12:
                    m = step - 2
                    z = stages[m]["z"]
                    b3 = nbank()
                    R.op("pe", lambda e, b3=b3, z=z: e.matmul(PS[:, b3, :], CM[:, 1, :], QN[:, z, :], start=True, stop=True),
                         reads=[("QN", z), ("CM",)], writes=[("ps", b3)])
                    R.op("dve", lambda e, z=z, s=s: e.tensor_tensor(out=T1[:, z, :], in0=QN[:, z, :], in1=CS[:, 0, sl(s)], op=ALU.mult),
                         reads=[("QN", z), ("CS",)], writes=[("T1", z)])
                    R.op("dve", lambda e, z=z, s=s, b3=b3: e.tensor_tensor(out=T2[:, z, :], in0=PS[:, b3, :], in1=CS[:, 1, sl(s)], op=ALU.mult),
                         reads=[("ps", b3), ("CS",)], writes=[("T2", z)])
                    R.op("pool", lambda e, z=z, s=s, m=m: e.tensor_tensor(out=QKST[:, s, m, :], in0=T1[:, z, :], in1=T2[:, z, :], op=ALU.add),
                         reads=[("T1", z), ("T2", z)], writes=[("QKST", s, m)])
            wv = wload(("v", l), 2048)
            for blk in range(4):
                b = nbank()
                for kc in range(8):
                    R.op("pe", lambda e, kc=kc, wv=wv, s=s, b=b, blk=blk: e.matmul(
                        PS[:, b, 0:256], A[:, kc, s * 512 + blk * 128:s * 512 + (blk + 1) * 128],
                        W[:, wv, kc * 256:(kc + 1) * 256], start=(kc == 0), stop=(kc == 7)),
                        reads=[("W", wv), ("A", kc, s)], writes=[("ps", b)])
                R.op("act", lambda e, s=s, b=b, blk=blk: e.activation(
                    out=VST[:, s * 4 + blk, :, 0:64], in_=PS[:, b, 0:256].rearrange("p (g d) -> p g d", g=4), func=AF.Copy),
                    reads=[("ps", b)], writes=[("VST", s, blk)])
            ts = t0 + s * 512
            d1 = R.dma("pool", qs[:, ts:ts + 512].rearrange("(m p) t -> p m t", p=128), QKST[:, s, 0:8, :],
                       reads=[("QKST", s, m) for m in range(8)], writes=[("qs", ts)])
            d2 = R.dma("pool", ks[:, ts:ts + 512].rearrange("(m p) t -> p m t", p=128), QKST[:, s, 8:10, :],
                       reads=[("QKST", s, m) for m in (8, 9)], writes=[("ks", ts)])
            R.dma("pool", vs[ts:ts + 512, :].rearrange("(b p) f -> p b f", p=128),
                  VSTF[:, s * 2048:(s + 1) * 2048].rearrange("p (b f) -> p b f", b=4),
                  reads=[("VST", s, blk) for blk in range(4)], writes=[("vs", ts)])
            R.add_reader(HKEYS, d1)
            R.add_reader(HKEYS, d2)

    def load_o(t0):
        R.dma("pool", Bz[:, :, :], os_[:, t0:t0 + T].rearrange("(kc p) t -> p kc t", p=128),
              reads=[("os", t0)], writes=[("B", kc, s) for kc in range(8) for s in range(NSUB)])

    def wo(l, t0):

        def ev(m, s, b):
            R.op("dve", lambda e: e.tensor_tensor(out=X[:, m, sl(s)], in0=PS[:, b, :], in1=X[:, m, sl(s)], op=ALU.add),
                 reads=[("ps", b), ("X", m, s)], writes=[("X", m, s)])
        linear8(lambda m: ("wo", l, m), Bz, "B", ev)

    def pool_in(l, t0):
        def ev(m, s, b):
            R.op("act", lambda e: e.activation(out=U[:, m, sl(s)], in_=PS[:, b, :], func=AF.Copy),
                 reads=[("ps", b)], writes=[("U", m, s)])
        linear8(lambda m: ("win", l, m), A, "A", ev)
        d = R.dma("pool", us[:, t0:t0 + T].rearrange("(kc p) t -> p kc t", p=128), U[:, :, :],
                  reads=[("U", m, s) for m in range(8) for s in range(NSUB)], writes=[("us", t0)])
        R.add_reader(HKEYS, d)

    BKEYS = [("B", kc, s) for kc in range(8) for s in range(NSUB)]

    def load_uh(t0, s0, L):
        first = (t0 == s0)
        last = (t0 + T == s0 + L)
        lo = t0 - (0 if first else 8)
        hi = t0 + T + (0 if last else 8)
        c0 = 8 - (t0 - lo)
        R.dma("pool", UH[:, :, c0:c0 + (hi - lo)], us[:, lo:hi].rearrange("(kc p) t -> p kc t", p=128),
              reads=[("us", "all")], writes=[("UH",)])
        if first:
            R.op("dve", lambda e: e.memset(UH[:, :, 0:8], 0.0), writes=[("UHl",)])
        if last:
            R.op("dve", lambda e: e.memset(UH[:, :, T + 8:T + 16], 0.0), writes=[("UHr",)])

    def pool_mix(l, t0, s0, L):
        first = (t0 == s0)
        last = (t0 + T == s0 + L)
        NCOL = T + 16
        for g, w in enumerate(POOLW):
            half = w // 2
            ch = slice(2 * g, 2 * g + 2)
            src = UH
            bufs = [TA, TB]
            n = NCOL
            k = 1
            cur = None
            idx = 0
            while k < w:
                n = n - k
                dst = bufs[idx % 2]
                if cur is None:
                    R.op("dve", lambda e, dst=dst, n=n, k=k, ch=ch: e.tensor_tensor(
                        out=dst[:, :, 0:n], in0=UH[:, ch, 0:n], in1=UH[:, ch, k:k + n], op=ALU.add),
                        reads=[("UH",), ("UHl",), ("UHr",)], writes=[("TT", idx % 2)] + BKEYS)
                else:
                    R.op("dve", lambda e, dst=dst, cur=cur, n=n, k=k: e.tensor_tensor(
                        out=dst[:, :, 0:n], in0=cur[:, :, 0:n], in1=cur[:, :, k:k + n], op=ALU.add),
                        reads=[("TT", (idx - 1) % 2)], writes=[("TT", idx % 2)] + BKEYS)
                cur = dst
                idx += 1
                k *= 2
            sb = 8 - half
            tk = ("TT", (idx - 1) % 2)
            R.op("dve", lambda e, cur=cur, sb=sb, w=w, ch=ch: e.scalar_tensor_tensor(
                out=A[:, ch, :], in0=cur[:, :, sb:sb + T], scalar=1.0 / w, in1=UH[:, ch, 8:8 + T], op0=ALU.mult, op1=ALU.subtract),
                reads=[tk, ("UH",)], writes=[("A", kc, s) for kc in (2 * g, 2 * g + 1) for s in range(NSUB)])
            fix = []
            if first:
                fix += [(t, t + half) for t in range(half)]
            if last:
                fix += [(t, (T - t) + half) for t in range(T - half + 1, T)]
            for (t, cnt) in fix:
                R.op("dve", lambda e, cur=cur, sb=sb, t=t, cnt=cnt, ch=ch: e.scalar_tensor_tensor(
                    out=A[:, ch, t:t + 1], in0=cur[:, :, sb + t:sb + t + 1], scalar=1.0 / cnt, in1=UH[:, ch, 8 + t:9 + t],
                    op0=ALU.mult, op1=ALU.subtract),
                    reads=[tk, ("UH",)], writes=[("A", kc, t // 512) for kc in (2 * g, 2 * g + 1)])
        wg = wload(("wg", l), 2048)
        for g in range(4):
            for m2 in range(2):
                for s in range(NSUB):
                    b = nbank()
                    for k2 in range(2):
                        c = ((g * 2 + k2) * 2 + m2) * 128
                        R.op("pe", lambda e, c=c, wg=wg, g=g, k2=k2, s=s, b=b: e.matmul(
                            PS[:, b, :], W[:, wg, c:c + 128], A[:, 2 * g + k2, sl(s)], start=(k2 == 0), stop=(k2 == 1)),
                            reads=[("W", wg), ("A", 2 * g + k2, s)], writes=[("ps", b)])
                    R.op("act", lambda e, g=g, m2=m2, s=s, b=b: e.activation(out=Bz[:, 2 * g + m2, sl(s)], in_=PS[:, b, :], func=AF.Copy),
                         reads=[("ps", b)], writes=[("B", 2 * g + m2, s)] + ([("TT", 0), ("TT", 1)] if (g, m2, s) == (0, 0, 0) else []))

        def ev(m, s, b):
            R.op("dve", lambda e: e.scalar_tensor_tensor(
                out=X[:, m, sl(s)], in0=PS[:, b, :], scalar=SM[:, COL_PS + l * 8 + m:COL_PS + l * 8 + m + 1], in1=X[:, m, sl(s)],
                op0=ALU.mult, op1=ALU.add),
                reads=[("ps", b), ("X", m, s), ("SM",)], writes=[("X", m, s)])
        linear8(lambda m: ("wout", l, m), Bz, "B", ev)

    def attention():
        R.op("dve", lambda e: e.memset(QZ[:, :, :, :], 0.0), writes=[("QZ", 0), ("QZ", 1)])
        s0 = 0
        it = 0
        for L in seqs:
            nkt = L // 128
            for g in range(4):
                for hb in (0, 64):
                    R.dma("sp", KD[hb:hb + 64, g, 0:L], ks[g * 64:(g + 1) * 64, s0:s0 + L], reads=[("ks", "all")], writes=[("KD", g)])
            step = 16
            for k0 in range(0, nkt, step):
                k1 = min(nkt, k0 + step)
                R.dma("sp", VA[:, k0:k1, :, :].rearrange("p k g d -> p k (g d)"),
                      vs[s0 + k0 * 128:s0 + k1 * 128, :].rearrange("(k p) f -> p k f", p=128),
                      reads=[("vs", "all")], writes=[("VA",)])
            iters = [(g, hp, qt) for g in range(4) for hp in range(2) for qt in range(L // 512)]

            def load_q(k, s0=s0):
                g, hp, qt = iters[k]
                c = 2 * g + hp
                tq = s0 + qt * 512
                for r in range(2):
                    R.dma("sp", QZ[r * 64:(r + 1) * 64, (it0 + k) % 2, r, :], qs[c * 128 + r * 64:c * 128 + (r + 1) * 64, tq:tq + 512],
                          reads=[("qs", "all")], writes=[("QZ", (it0 + k) % 2)])

            it0 = it
            load_q(0)
            for k, (g, hp, qt) in enumerate(iters):
                c = 2 * g + hp
                tq = s0 + qt * 512
                qsl = it % 2
                it += 1
                if k + 1 < len(iters):
                    load_q(k + 1)
                LA = 2

                def QK(i, g=g, qsl=qsl):
                    j = i % 3
                    for r in range(2):
                        sb = 2 + 2 * j + r
                        R.op("pe", lambda e, r=r, sb=sb: e.matmul(PS[:, sb, :], KD[:, g, i * 128:(i + 1) * 128],
                                                                 QZ[:, qsl, r, :], start=True, stop=True),
                             reads=[("KD", g), ("QZ", qsl)], writes=[("ps", sb)])
                    R.op("act", lambda e: e.activation(out=PT[:, j, :], in_=PS[:, 2 + 2 * j:4 + 2 * j, :].rearrange("p a b -> p (a b)"), func=AF.Exp),
                         reads=[("ps", 2 + 2 * j), ("ps", 3 + 2 * j)], writes=[("PT", j)])

                def PV(i, g=g, nkt=nkt):
                    j = i % 3
                    for r in range(2):
                        R.op("pe", lambda e, r=r: e.matmul(PS[:, r, :], VA[:, i, g, :], PT[:, j, r * 512:(r + 1) * 512],
                                                           start=(i == 0), stop=(i == nkt - 1)),
                             reads=[("VA",), ("PT", j)], writes=[("ps", r)])

                for i in range(nkt + LA):
                    if i < nkt:
                        QK(i)
                    if i >= LA:
                        PV(i - LA)
                for r in range(2):
                    R.op("dve", lambda e, r=r, qsl=qsl: e.tensor_copy(out=OC[0:64, qsl, r, :], in_=PS[0:64, r, :]),
                         reads=[("ps", r)], writes=[("OC", qsl, r)])
                    R.op("dve", lambda e, r=r, qsl=qsl: e.tensor_copy(out=DC[0:64, qsl, r, :], in_=PS[64:128, r, :]),
                         reads=[("ps", r)], writes=[("DC", qsl, r)])
                for r in range(2):
                    R.op("dve", lambda e, r=r, qsl=qsl: e.reciprocal(out=REC[0:64, r, :], in_=DC[0:64, qsl, r, :]),
                         reads=[("DC", qsl, r)], writes=[("REC", r)])
                    R.op("dve", lambda e, r=r, qsl=qsl: e.tensor_tensor(
                        out=OST[r * 64:(r + 1) * 64, qsl, :], in0=OC[0:64, qsl, r, :], in1=REC[0:64, r, :], op=ALU.mult),
                        reads=[("OC", qsl, r), ("REC", r)], writes=[("OST", qsl)])
                R.dma("pool", os_[c * 128:(c + 1) * 128, tq:tq + 512], OST[:, qsl, :],
                      reads=[("OST", qsl)], writes=[("os", tq)])
            s0 += L

    def phase_ok(p):
        return phases is None or p in phases

    def vst_ones():
        R.op("dve", lambda e: e.memset(VSTF[:, :], 1.0), writes=[("VST", s, blk) for s in range(NSUB) for blk in range(4)])

    vst_ones()
    def nxt(k):
        return tiles[k + 1][0] if k + 1 < len(tiles) else None

    def load_cs(pos0):
        R.dma("pool", CS[:, :, :], rope[:, :, pos0:pos0 + T].rearrange("a p t -> p a t"), writes=[("CS",)])

    if phase_ok(1):
        load_x(xT, tiles[0][0])
        for k, (t0, s0, L) in enumerate(tiles):
            load_cs(t0 - s0)
            norm(0); ffn(0, 0)
            norm(1)
            store_x(xs, t0, "xs")
            if nxt(k) is not None:
                load_x(xT, nxt(k))
            qkv(0, t0, t0 - s0)
        R.barrier()
    if phase_ok(2):
        attention()
        R.barrier()
    if phase_ok(3):
        load_x(xs, tiles[0][0])
        load_o(tiles[0][0])
        for k, (t0, s0, L) in enumerate(tiles):
            wo(0, t0)
            if nxt(k) is not None:
                load_o(nxt(k))
            norm(2); ffn(0, 1)
            norm(3); ffn(1, 0)
            norm(4)
            store_x(xs, t0, "xs")
            if nxt(k) is not None:
                load_x(xs, nxt(k))
            pool_in(0, t0)
        R.barrier()
    if phase_ok(4):
        load_x(xs, tiles[0][0])
        load_uh(*tiles[0])
        for k, (t0, s0, L) in enumerate(tiles):
            load_cs(t0 - s0)
            pool_mix(0, t0, s0, L)
            if k + 1 < len(tiles):
                load_uh(*tiles[k + 1])
            norm(5); ffn(1, 1)
            norm(6); ffn(2, 0)
            norm(7)
            store_x(xs, t0, "xs")
            if nxt(k) is not None:
                load_x(xs, nxt(k))
            qkv(1, t0, t0 - s0)
        R.barrier()
    if phase_ok(5):
        attention()
        R.barrier()
    if phase_ok(6):
        load_x(xs, tiles[0][0])
        load_o(tiles[0][0])
        for k, (t0, s0, L) in enumerate(tiles):
            wo(1, t0)
            if nxt(k) is not None:
                load_o(nxt(k))
            norm(8); ffn(2, 1)
            norm(9); ffn(3, 0)
            norm(10)
            store_x(xs, t0, "xs")
            if nxt(k) is not None:
                load_x(xs, nxt(k))
            pool_in(1, t0)
        R.barrier()
    if phase_ok(7):
        load_uh(*tiles[0])
        load_x(xs, tiles[0][0])
        for k, (t0, s0, L) in enumerate(tiles):
            pool_mix(1, t0, s0, L)
            if k + 1 < len(tiles):
                load_uh(*tiles[k + 1])
            norm(11); ffn(3, 1)
            for m in range(8):
                R.dma("pool", yT[m * 128:(m + 1) * 128, t0:t0 + T], X[:, m, :],
                      reads=[("X", m, s) for s in range(NSUB)], writes=[("y", t0, m)])
            if nxt(k) is not None:
                for m in range(8):
                    R.dma("pool", X[:, m, :], xs[m * 128:(m + 1) * 128, nxt(k):nxt(k) + T],
                          reads=[("dX", nxt(k))], writes=[("X", m, s) for s in range(NSUB)])
        R.barrier()
    elif phases is not None:
        for (t0, s0, L) in tiles:
            load_x(xs, t0)
            store_x(yT, t0, "y")
        R.barrier()

    nsem = R.finalize()
    from contextlib import ExitStack
    with ExitStack() as es:
        sems = {}
        for e in Rec.ENGS:
            for ep in range(nsem[e]):
                sems[(e, ep)] = es.enter_context(nc.semaphore(f"s_{e}_{ep}"))
        for i in range(NDMASEM):
            sems[("dma", i)] = es.enter_context(nc.semaphore(f"s_dma_{i}"))
        block = es.enter_context(nc.Block())

        def replay(name, eng):
            waited = {}
            for o in R.streams[name]:
                for d in o.deps:
                    if waited.get(d.sem, 0) >= d.val:
                        continue
                    eng.wait_ge(sems[d.sem], d.val)
                    waited[d.sem] = d.val
                ins = o.fn(eng)
                if o.sig:
                    ins.then_inc(sems[o.sem], 16 if o.is_dma else 1)
            for d in R.pending[name]:
                if d.sem is None or waited.get(d.sem, 0) >= d.val:
                    continue
                eng.wait_ge(sems[d.sem], d.val)
                waited[d.sem] = d.val

        @block.sync
        def _(e):
            replay("sp", e)

        @block.scalar
        def _(e):
            replay("act", e)

        @block.vector
        def _(e):
            replay("dve", e)

        @block.gpsimd
        def _(e):
            replay("pool", e)

        @block.tensor
        def _(e):
            replay("pe", e)
    return nc


def make_in_maps(inputs, seqs, xts):
    wall = pack_weights(inputs)
    sm = pack_small(inputs)
    cm = const_mats()
    rp = rope_tables()
    return [{"xT": x, "wall": wall, "small": sm, "cmat": cm, "rope": rp} for x in xts]


def kernel(**inputs):
    inputs = {k: np.asarray(v) for k, v in inputs.items()}
    xp = inputs["x_prompt"]
    xsm = inputs["x_sample"]
    n = 8
    xts = []
    for c in range(n):
        rows = np.concatenate([xp[2 * c], xp[2 * c + 1], xsm[c]], axis=0)
        xts.append(np.ascontiguousarray(rows.T))
    nc = build_program(SEQS_FULL)
    in_maps = make_in_maps(inputs, SEQS_FULL, xts)
    res = run_bass_kernel_spmd(nc, in_maps, core_ids=list(range(n)))
    yp = np.empty_like(xp)
    ysm = np.empty_like(xsm)
    for c in range(n):
        y = np.asarray(res.results[c]["yT"]).T
        yp[2 * c] = y[0:4096]
        yp[2 * c + 1] = y[4096:8192]
        ysm[c] = y[8192:16384]
    return (yp, ysm)
```

```python
import numpy as np
import concourse.bass as bass
import concourse.mybir as mybir
from concourse.bass_utils import run_bass_kernel_spmd

F32 = mybir.dt.float32
BF16 = mybir.dt.bfloat16
ALU = mybir.AluOpType
AF = mybir.ActivationFunctionType

D = 1024
DFF = 2816
NFC = 22
NKC = 8
T = 1024
NSUB = T // 512
EPS = 1e-6
DEPTH = 4
POOLW = (2, 4, 8, 16)
SMAX = 8192
SEQS_FULL = (4096, 4096, 8192)
EPOCH = 16000
NDMASEM = 24
CAST_CHUNK = 1 << 21


def weight_layout():
    off = {}
    cur = 0

    def add(key, n):
        nonlocal cur
        off[key] = cur
        cur += n

    def ffn(i, j):
        for fc in range(NFC):
            add(("gu", i, j, fc), 128 * 2048)
        for m in range(8):
            add(("dn", i, j, m), 128 * 2816)

    def attn_qkv(l):
        for m in range(10):
            add(("qk", l, m), 128 * 1024)
        add(("v", l), 128 * 2048)

    def attn_o(l):
        for m in range(8):
            add(("wo", l, m), 128 * 1024)

    def pool_in(l):
        for m in range(8):
            add(("win", l, m), 128 * 1024)

    def pool_rest(l):
        add(("wg", l), 128 * 2048)
        for m in range(8):
            add(("wout", l, m), 128 * 1024)

    ffn(0, 0); attn_qkv(0); attn_o(0); ffn(0, 1)
    ffn(1, 0); pool_in(0); pool_rest(0); ffn(1, 1)
    ffn(2, 0); attn_qkv(1); attn_o(1); ffn(2, 1)
    ffn(3, 0); pool_in(1); pool_rest(1); ffn(3, 1)
    return off, cur


def pack_weights(inp):
    off, total = weight_layout()
    wall = np.empty(total, np.float32)

    def put(key, arr):
        a = np.ascontiguousarray(arr, dtype=np.float32).reshape(-1)
        wall[off[key]:off[key] + a.size] = a

    for i in range(DEPTH):
        for j in range(2):
            wg = inp["ffn_w_gate"][i, j].reshape(8, 128, NFC, 128)
            wu = inp["ffn_w_up"][i, j].reshape(8, 128, NFC, 128)
            gu = np.stack([wg, wu], 0).transpose(3, 2, 0, 1, 4)
            for fc in range(NFC):
                put(("gu", i, j, fc), gu[fc])
            wd = inp["ffn_w_down"][i, j].reshape(NFC, 128, 8, 128).transpose(2, 1, 0, 3)
            for m in range(8):
                put(("dn", i, j, m), wd[m])
    for l in range(2):
        wq = inp["attn_w_qkv"][l]
        qk = wq[:, :1280].reshape(8, 128, 10, 128).transpose(2, 1, 0, 3)
        for m in range(10):
            put(("qk", l, m), qk[m])
        put(("v", l), wq[:, 1280:].reshape(8, 128, 256).transpose(1, 0, 2))
        wo = inp["attn_w_o"][l].reshape(8, 128, 8, 128).transpose(2, 1, 0, 3)
        for m in range(8):
            put(("wo", l, m), wo[m])
        wi = inp["pool_w_in"][l].reshape(8, 128, 8, 128).transpose(2, 1, 0, 3)
        wt = inp["pool_w_out"][l].reshape(8, 128, 8, 128).transpose(2, 1, 0, 3)
        for m in range(8):
            put(("win", l, m), wi[m])
            put(("wout", l, m), wt[m])
        put(("wg", l), inp["pool_w_group"][l].reshape(4, 2, 128, 2, 128).transpose(2, 0, 1, 3, 4))
    return wall


COL_G = 0
COL_PS = 96
COL_QG = 112
COL_KG = 114
COL_QG8 = 116
NS = 120


def pack_small(inp):
    sm = np.zeros((128, NS), np.float32)
    g = inp["norm_gains"].reshape(12, 8, 128)
    sm[:, COL_G:COL_G + 96] = g.transpose(2, 0, 1).reshape(128, 96)
    ps = inp["pool_scale"].reshape(2, 8, 128)
    sm[:, COL_PS:COL_PS + 16] = ps.transpose(2, 0, 1).reshape(128, 16)
    for l in range(2):
        sm[:, COL_QG + l] = np.tile(inp["attn_q_gain"][l], 2)
        sm[:, COL_KG + l] = np.tile(inp["attn_k_gain"][l], 2)
    return sm


def const_mats():
    cm = np.zeros((2, 128, 128), np.float32)
    cm[0, :64, :64] = 1.0
    cm[0, 64:, 64:] = 1.0
    for m in range(128):
        cm[1, m ^ 1, m] = 1.0
    return cm


def rope_tables():
    t = np.arange(SMAX)
    row = (t // 64).astype(np.float32)
    col = (t % 64).astype(np.float32)
    freqs = (np.float32(10000.0) ** (-np.arange(16, dtype=np.float32) / np.float32(16))).astype(np.float32)
    ang = np.concatenate([row[:, None] * freqs, col[:, None] * freqs], -1).astype(np.float32)
    cos = np.cos(ang).astype(np.float32)
    sin = np.sin(ang).astype(np.float32)
    tab = np.zeros((2, 128, SMAX), np.float32)
    for p in range(128):
        d = p % 64
        i = d // 2
        tab[0, p] = cos[:, i]
        tab[1, p] = -sin[:, i] if d % 2 == 0 else sin[:, i]
    return tab


class Op:
    __slots__ = ("eng", "fn", "deps", "sig", "is_dma", "sem", "val", "idx")

    def __init__(self, eng, fn, is_dma=False):
        self.eng = eng
        self.fn = fn
        self.deps = []
        self.sig = False
        self.is_dma = is_dma
        self.sem = None
        self.val = 0


class Rec:
    ENGS = ("sp", "act", "dve", "pool", "pe")

    def __init__(self):
        self.streams = {e: [] for e in self.ENGS}
        self.track = {}
        self.last_dma_on_sem = [None] * NDMASEM
        self.dma_cnt = [0] * NDMASEM
        self.dma_rr = 0
        self.dma_rrq = [0, 0]
        self.pending = {e: [] for e in self.ENGS}

    def _dep(self, op, d, war):
        if d is None or d is op:
            return
        if not d.is_dma and not op.is_dma and d.eng == op.eng:
            if op.eng == "pe":
                return
            if war:
                return
        d.sig = True
        op.deps.append(d)

    def op(self, eng, fn, reads=(), writes=(), is_dma=False, extra=()):
        o = Op(eng, fn, is_dma)
        for d in self.pending[eng]:
            self._dep(o, d, False)
        self.pending[eng] = []
        for d in extra:
            self._dep(o, d, False)
        for k in reads:
            st = self.track.get(k)
            if st is not None:
                self._dep(o, st[0], False)
        for k in writes:
            st = self.track.get(k)
            if st is not None:
                self._dep(o, st[0], True)
                for r in st[1]:
                    self._dep(o, r, True)
        for k in reads:
            st = self.track.get(k)
            if st is None:
                self.track[k] = [None, [o]]
            else:
                st[1].append(o)
        for k in writes:
            self.track[k] = [o, []]
        if is_dma:
            half = NDMASEM // 2
            qi = 0 if eng == "sp" else 1
            s = qi * half + self.dma_rrq[qi]
            self.dma_rrq[qi] = (self.dma_rrq[qi] + 1) % half
            prev = self.last_dma_on_sem[s]
            if prev is not None:
                o.deps.append(prev)
            self.dma_cnt[s] += 1
            o.sem = ("dma", s)
            o.val = 16 * self.dma_cnt[s]
            o.sig = True
            self.last_dma_on_sem[s] = o
        self.streams[eng].append(o)
        return o

    def dma(self, eng, out, in_, reads=(), writes=()):
        return self.op(eng, lambda e: e.dma_start(out=out, in_=in_), reads, writes, is_dma=True)

    def all_deps(self):
        deps = []
        for e in self.ENGS:
            for o in reversed(self.streams[e]):
                if not o.is_dma:
                    deps.append(o)
                    break
        for o in self.last_dma_on_sem:
            if o is not None:
                deps.append(o)
        return deps

    def add_reader(self, keys, o):
        for k in keys:
            st = self.track.get(k)
            if st is None:
                self.track[k] = [None, [o]]
            else:
                st[1].append(o)

    def barrier(self):
        deps = []
        for e in self.ENGS:
            for o in reversed(self.streams[e]):
                if not o.is_dma:
                    deps.append(o)
                    break
        for o in self.last_dma_on_sem:
            if o is not None:
                deps.append(o)
        for e in self.ENGS:
            self.pending[e] = list(deps)
        self.track = {}

    def finalize(self):
        nsem = {}
        for e in self.ENGS:
            ep = 0
            cnt = 0
            for o in self.streams[e]:
                if o.is_dma or not o.sig:
                    continue
                if cnt >= EPOCH:
                    ep += 1
                    cnt = 0
                cnt += 1
                o.sem = (e, ep)
                o.val = cnt
            nsem[e] = ep + 1
        return nsem


def tile_info(seqs):
    out = []
    s0 = 0
    for L in seqs:
        for a in range(0, L, T):
            out.append((s0 + a, s0, L))
        s0 += L
    return out


def build_program(seqs=SEQS_FULL, phases=None):
    NTOK = sum(seqs)
    tiles = tile_info(seqs)
    woff, NW = weight_layout()
    nc = bass.Bass("TRN2", target_bir_lowering=False)

    xT = nc.dram_tensor("xT", [D, NTOK], F32, kind="ExternalInput").ap()
    wall = nc.dram_tensor("wall", [NW], F32, kind="ExternalInput").ap()
    small = nc.dram_tensor("small", [128, NS], F32, kind="ExternalInput").ap()
    cmat = nc.dram_tensor("cmat", [2, 128, 128], F32, kind="ExternalInput").ap()
    rope = nc.dram_tensor("rope", [2, 128, SMAX], F32, kind="ExternalInput").ap()
    yT = nc.dram_tensor("yT", [D, NTOK], F32, kind="ExternalOutput").ap()
    wbf = nc.dram_tensor("wbf", [NW], BF16).ap()
    xs = nc.dram_tensor("xs", [D, NTOK], F32).ap()
    us = nc.dram_tensor("us", [D, NTOK], F32).ap()
    qs = nc.dram_tensor("qs", [D, NTOK], BF16).ap()
    ks = nc.dram_tensor("ks", [256, NTOK], BF16).ap()
    vs = nc.dram_tensor("vs", [NTOK, 512], BF16).ap()
    os_ = nc.dram_tensor("os", [D, NTOK], BF16).ap()

    base = [17408]

    def alloc(name, shape, dt, at=None):
        n = int(np.prod(shape[1:])) * (4 if dt == F32 else 2)
        n = (n + 63) // 64 * 64
        if at is None:
            at = base[0]
            base[0] += n
        assert at + n <= 229376 - 512, (name, at, n)
        return nc.alloc_sbuf_tensor_at(name, list(shape), dt, offset=at)

    SM = alloc("SM", [128, NS], F32)
    CM = alloc("CM", [128, 2, 128], F32)
    ONESB = alloc("ONESB", [128, 128], BF16)
    EPSC = alloc("EPSC", [128, 16], F32)
    vst_at = base[0]
    VST = alloc("VST", [128, 8, 4, 128], BF16)
    VSTF = alloc("VSTF", [128, 4096], BF16, at=vst_at)
    common_end = base[0]
    X = alloc("X", [128, 8, T], F32)
    A = alloc("A", [128, 8, T], BF16)
    B = alloc("B", [128, 2, T + 16, 2], F32)
    H = alloc("H", [128, NFC, T], BF16)
    NW_SLOT = 5
    W = alloc("W", [128, NW_SLOT, 2816], BF16)
    CS = alloc("CS", [128, 2, T], F32)
    SQ = alloc("SQ", [128, 4, 512], BF16)
    SD = alloc("SD", [128, 2, 512], F32)
    RS = alloc("RS", [128, 2, 512], F32)
    SG = alloc("SG", [128, 3, 512], F32)
    UH = alloc("UH", [128, 8, T + 16], F32)
    tl_end = base[0]
    offs = {}
    cur = common_end
    for nm, nbytes in (("X", 8 * T * 4), ("A", 8 * T * 2), ("B", 2 * (T + 16) * 2 * 4), ("H", NFC * T * 2)):
        offs[nm] = cur
        cur += (nbytes + 63) // 64 * 64
    Bz = alloc("Bz", [128, 8, T], BF16, at=offs["B"])
    TA = alloc("TA", [128, 2, T + 16], F32, at=offs["B"])
    TB = alloc("TB", [128, 2, T + 16], F32, at=offs["B"] + 2 * (T + 16) * 4)
    U = alloc("U", [128, 8, T], F32, at=offs["H"])
    hq = offs["H"]
    QKST = alloc("QKST", [128, 2, 10, 512], BF16, at=hq); hq += 2 * 10 * 512 * 2
    SQ32 = alloc("SQ32", [128, 2, 512], F32, at=hq); hq += 2 * 512 * 4
    QN = alloc("QN", [128, 2, 512], F32, at=hq); hq += 2 * 512 * 4
    T1 = alloc("T1", [128, 2, 512], F32, at=hq); hq += 2 * 512 * 4
    T2 = alloc("T2", [128, 2, 512], F32, at=hq); hq += 2 * 512 * 4
    assert hq <= offs["H"] + NFC * T * 2
    cur = common_end
    smx = max(seqs)
    KD = alloc("KD", [128, 4, smx], BF16, at=cur); cur += 4 * smx * 2
    VA = alloc("VA", [128, smx // 128, 4, 128], BF16, at=cur); cur += (smx // 128) * 512 * 2
    QZ = alloc("QZ", [128, 2, 2, 512], BF16, at=cur); cur += 2 * 2 * 512 * 2
    PT = alloc("PT", [128, 3, 1024], BF16, at=cur); cur += 3 * 1024 * 2
    OST = alloc("OST", [128, 2, 512], BF16, at=cur); cur += 2 * 512 * 2
    REC = alloc("REC", [128, 2, 512], F32, at=cur); cur += 2 * 512 * 4
    OC = alloc("OC", [128, 2, 2, 512], F32, at=cur); cur += 2 * 2 * 512 * 4
    DC = alloc("DC", [128, 2, 2, 512], F32, at=cur); cur += 2 * 2 * 512 * 4
    assert cur <= 229376 - 512, cur

    PS = nc.alloc_psum_tensor("PS", [128, 8, 512], F32)

    R = Rec()
    bank_rr = [0]

    def nbank():
        b = bank_rr[0]
        bank_rr[0] = (b + 1) % 8
        return b

    ring = {"W": 0, "SQ": 0, "SG": 0}

    def nslot(name, n):
        s = ring[name]
        ring[name] = (s + 1) % n
        return s

    def sl(s):
        return slice(s * 512, (s + 1) * 512)

    HKEYS = [("H", fc, s) for fc in range(NFC) for s in range(NSUB)]

    R.dma("sp", SM[:, :], small[:, :], writes=[("SM",)])
    R.dma("sp", CM[:, :, :], cmat.rearrange("a p c -> p a c"), writes=[("CM",)])
    R.op("dve", lambda e: e.memset(ONESB[:, :], 1.0), writes=[("ONESB",)])
    R.op("dve", lambda e: e.memset(EPSC[:, :], EPS), writes=[("EPSC",)])
    R.op("dve", lambda e: e.tensor_scalar(SM[:, COL_QG8:COL_QG8 + 2], SM[:, COL_QG:COL_QG + 2], 0.125, None, op0=ALU.mult),
         reads=[("SM",)], writes=[("SMq",)])
    nchunk = (NW + CAST_CHUNK - 1) // CAST_CHUNK
    for c in range(nchunk):
        a, b = c * CAST_CHUNK, min(NW, (c + 1) * CAST_CHUNK)
        R.dma("pool", wbf[a:b].rearrange("(r c) -> r c", c=2048), wall[a:b].rearrange("(r c) -> r c", c=2048),
              writes=[("wbf", c)])

    def wload(key, L):
        o = woff[key]
        slot = nslot("W", NW_SLOT)
        c0, c1 = o // CAST_CHUNK, (o + 128 * L - 1) // CAST_CHUNK
        R.dma("sp", W[:, slot, 0:L], wbf[o:o + 128 * L].rearrange("(p l) -> p l", p=128),
              reads=[("wbf", c) for c in range(c0, c1 + 1)], writes=[("W", slot)])
        return slot

    def norm(n):
        for s in range(NSUB):
            b = nbank()
            for kc in range(8):
                q = nslot("SQ", 4)
                R.op("act", lambda e, kc=kc, q=q, s=s: e.activation(out=SQ[:, q, :], in_=X[:, kc, sl(s)], func=AF.Square),
                     reads=[("X", kc, s)], writes=[("SQ", q)])
                R.op("pe", lambda e, kc=kc, q=q, b=b: e.matmul(PS[:, b, :], ONESB[:, :], SQ[:, q, :], start=(kc == 0), stop=(kc == 7)),
                     reads=[("SQ", q), ("ONESB",)], writes=[("ps", b)])
            R.op("act", lambda e, b=b, s=s: e.activation(out=SD[:, s, :], in_=PS[:, b, :], func=AF.Ln, scale=1.0 / D, bias=EPSC[:, 0:1]),
                 reads=[("ps", b), ("EPSC",)], writes=[("SD", s)])
            R.op("act", lambda e, s=s: e.activation(out=RS[:, s, :], in_=SD[:, s, :], func=AF.Exp, scale=-0.5),
                 reads=[("SD", s)], writes=[("RS", s)])
            for kc in range(8):
                R.op("dve", lambda e, kc=kc, s=s: e.scalar_tensor_tensor(
                    out=A[:, kc, sl(s)], in0=X[:, kc, sl(s)], scalar=SM[:, COL_G + n * 8 + kc:COL_G + n * 8 + kc + 1],
                    in1=RS[:, s, :], op0=ALU.mult, op1=ALU.mult),
                    reads=[("X", kc, s), ("RS", s), ("SM",)], writes=[("A", kc, s)])

    def ffn(i, j):
        for fc in range(NFC):
            w = wload(("gu", i, j, fc), 2048)
            for s in range(NSUB):
                bg, bu = nbank(), nbank()
                for kc in range(8):
                    R.op("pe", lambda e, kc=kc, w=w, s=s, bg=bg: e.matmul(
                        PS[:, bg, :], W[:, w, kc * 128:(kc + 1) * 128], A[:, kc, sl(s)], start=(kc == 0), stop=(kc == 7)),
                        reads=[("W", w), ("A", kc, s)], writes=[("ps", bg)])
                for kc in range(8):
                    R.op("pe", lambda e, kc=kc, w=w, s=s, bu=bu: e.matmul(
                        PS[:, bu, :], W[:, w, 1024 + kc * 128:1024 + (kc + 1) * 128], A[:, kc, sl(s)], start=(kc == 0), stop=(kc == 7)),
                        reads=[("W", w), ("A", kc, s)], writes=[("ps", bu)])
                g = nslot("SG", 3)
                R.op("act", lambda e, bg=bg, g=g: e.activation(out=SG[:, g, :], in_=PS[:, bg, :], func=AF.Silu),
                     reads=[("ps", bg)], writes=[("SG", g)])
                R.op("dve", lambda e, bu=bu, g=g, fc=fc, s=s: e.tensor_tensor(
                    out=H[:, fc, sl(s)], in0=PS[:, bu, :], in1=SG[:, g, :], op=ALU.mult),
                    reads=[("ps", bu), ("SG", g)], writes=[("H", fc, s)])
        for m in range(8):
            w = wload(("dn", i, j, m), 2816)
            for s in range(NSUB):
                b = nbank()
                for fc in range(NFC):
                    R.op("pe", lambda e, fc=fc, w=w, s=s, b=b: e.matmul(
                        PS[:, b, :], W[:, w, fc * 128:(fc + 1) * 128], H[:, fc, sl(s)], start=(fc == 0), stop=(fc == NFC - 1)),
                        reads=[("W", w), ("H", fc, s)], writes=[("ps", b)])
                R.op("dve", lambda e, m=m, s=s, b=b: e.scalar_tensor_tensor(
                    out=X[:, m, sl(s)], in0=PS[:, b, :], scalar=0.5, in1=X[:, m, sl(s)], op0=ALU.mult, op1=ALU.add),
                    reads=[("ps", b), ("X", m, s)], writes=[("X", m, s)])

    def linear8(keyf, src, srckey, evac):
        for m in range(8):
            w = wload(keyf(m), 1024)
            for s in range(NSUB):
                b = nbank()
                for kc in range(8):
                    R.op("pe", lambda e, kc=kc, w=w, s=s, b=b: e.matmul(
                        PS[:, b, :], W[:, w, kc * 128:(kc + 1) * 128], src[:, kc, sl(s)], start=(kc == 0), stop=(kc == 7)),
                        reads=[("W", w), (srckey, kc, s)], writes=[("ps", b)])
                evac(m, s, b)

    def load_x(src, t0):
        R.dma("pool", X[:, :, :], src[:, t0:t0 + T].rearrange("(kc p) t -> p kc t", p=128),
              reads=[("dX", t0)], writes=[("X", kc, s) for kc in range(8) for s in range(NSUB)])

    def store_x(dst, t0, key):
        return R.dma("pool", dst[:, t0:t0 + T].rearrange("(kc p) t -> p kc t", p=128), X[:, :, :],
                     reads=[("X", kc, s) for kc in range(8) for s in range(NSUB)], writes=[(key, t0)])

    def qkv(l, t0, pos0):
        wv = None
        for s in range(NSUB):
            stages = []
            for step in range(10 + 2):
                if step < 10:
                    m = step
                    w = wload(("qk", l, m), 1024)
                    b = nbank()
                    for kc in range(8):
                        R.op("pe", lambda e, kc=kc, w=w, s=s, b=b: e.matmul(
                            PS[:, b, :], W[:, w, kc * 128:(kc + 1) * 128], A[:, kc, sl(s)], start=(kc == 0), stop=(kc == 7)),
                            reads=[("W", w), ("A", kc, s)], writes=[("ps", b)])
                    z = m % 2
                    R.op("act", lambda e, b=b, z=z: e.activation(out=SQ32[:, z, :], in_=PS[:, b, :], func=AF.Square),
                         reads=[("ps", b)], writes=[("SQ32", z)])
                    stages.append({"b": b, "z": z})
                if 1 <= step < 11:
                    m = step - 1
                    st = stages[m]
                    b, z = st["b"], st["z"]
                    b2 = nbank()
                    R.op("pe", lambda e, b2=b2, z=z: e.matmul(PS[:, b2, :], CM[:, 0, :], SQ32[:, z, :], start=True, stop=True),
                         reads=[("SQ32", z), ("CM",)], writes=[("ps", b2)])
                    R.op("act", lambda e, b2=b2, z=z: e.activation(out=SD[:, z, :], in_=PS[:, b2, :], func=AF.Ln, scale=1.0 / 64, bias=EPSC[:, 0:1]),
                         reads=[("ps", b2), ("EPSC",)], writes=[("SD", z)])
                    R.op("act", lambda e, z=z: e.activation(out=RS[:, z, :], in_=SD[:, z, :], func=AF.Exp, scale=-0.5),
                         reads=[("SD", z)], writes=[("RS", z)])
                    gc = (COL_QG8 + l) if m < 8 else (COL_KG + l)
                    R.op("dve", lambda e, b=b, z=z, gc=gc: e.scalar_tensor_tensor(
                        out=QN[:, z, :], in0=PS[:, b, :], scalar=SM[:, gc:gc + 1], in1=RS[:, z, :], op0=ALU.mult, op1=ALU.mult),
                        reads=[("ps", b), ("RS", z), ("SM",), ("SMq",)], writes=[("QN", z)])
                if 2 <= step < 12:
                    m = step - 2
                    z = stages[m]["z"]
                    b3 = nbank()
                    R.op("pe", lambda e, b3=b3, z=z: e.matmul(PS[:, b3, :], CM[:, 1, :], QN[:, z, :], start=True, stop=True),
                         reads=[("QN", z), ("CM",)], writes=[("ps", b3)])
                    R.op("dve", lambda e, z=z, s=s: e.tensor_tensor(out=T1[:, z, :], in0=QN[:, z, :], in1=CS[:, 0, sl(s)], op=ALU.mult),
                         reads=[("QN", z), ("CS",)], writes=[("T1", z)])
                    R.op("dve", lambda e, z=z, s=s, b3=b3: e.tensor_tensor(out=T2[:, z, :], in0=PS[:, b3, :], in1=CS[:, 1, sl(s)], op=ALU.mult),
                         reads=[("ps", b3), ("CS",)], writes=[("T2", z)])
                    R.op("pool", lambda e, z=z, s=s, m=m: e.tensor_tensor(out=QKST[:, s, m, :], in0=T1[:, z, :], in1=T2[:, z, :], op=ALU.add),
                         reads=[("T1", z), ("T2", z)], writes=[("QKST", s, m)])
            wv = wload(("v", l), 2048)
            for blk in range(4):
                b = nbank()
                for kc in range(8):
                    R.op("pe", lambda e, kc=kc, wv=wv, s=s, b=b, blk=blk: e.matmul(
                        PS[:, b, 0:256], A[:, kc, s * 512 + blk * 128:s * 512 + (blk + 1) * 128],
                        W[:, wv, kc * 256:(kc + 1) * 256], start=(kc == 0), stop=(kc == 7)),
                        reads=[("W", wv), ("A", kc, s)], writes=[("ps", b)])
                R.op("act", lambda e, s=s, b=b, blk=blk: e.activation(
                    out=VST[:, s * 4 + blk, :, 0:64], in_=PS[:, b, 0:256].rearrange("p (g d) -> p g d", g=4), func=AF.Copy),
                    reads=[("ps", b)], writes=[("VST", s, blk)])
            ts = t0 + s * 512
            d1 = R.dma("pool", qs[:, ts:ts + 512].rearrange("(m p) t -> p m t", p=128), QKST[:, s, 0:8, :],
                       reads=[("QKST", s, m) for m in range(8)], writes=[("qs", ts)])
            d2 = R.dma("pool", ks[:, ts:ts + 512].rearrange("(m p) t -> p m t", p=128), QKST[:, s, 8:10, :],
                       reads=[("QKST", s, m) for m in (8, 9)], writes=[("ks", ts)])
            R.dma("pool", vs[ts:ts + 512, :].rearrange("(b p) f -> p b f", p=128),
                  VSTF[:, s * 2048:(s + 1) * 2048].rearrange("p (b f) -> p b f", b=4),
                  reads=[("VST", s, blk) for blk in range(4)], writes=[("vs", ts)])
            R.add_reader(HKEYS, d1)
            R.add_reader(HKEYS, d2)

    def load_o(t0):
        R.dma("pool", Bz[:, :, :], os_[:, t0:t0 + T].rearrange("(kc p) t -> p kc t", p=128),
              reads=[("os", t0)], writes=[("B", kc, s) for kc in range(8) for s in range(NSUB)])

    def wo(l, t0):

        def ev(m, s, b):
            R.op("dve", lambda e: e.tensor_tensor(out=X[:, m, sl(s)], in0=PS[:, b, :], in1=X[:, m, sl(s)], op=ALU.add),
                 reads=[("ps", b), ("X", m, s)], writes=[("X", m, s)])
        linear8(lambda m: ("wo", l, m), Bz, "B", ev)

    def pool_in(l, t0):
        def ev(m, s, b):
            R.op("act", lambda e: e.activation(out=U[:, m, sl(s)], in_=PS[:, b, :], func=AF.Copy),
                 reads=[("ps", b)], writes=[("U", m, s)])
        linear8(lambda m: ("win", l, m), A, "A", ev)
        d = R.dma("pool", us[:, t0:t0 + T].rearrange("(kc p) t -> p kc t", p=128), U[:, :, :],
                  reads=[("U", m, s) for m in range(8) for s in range(NSUB)], writes=[("us", t0)])
        R.add_reader(HKEYS, d)

    BKEYS = [("B", kc, s) for kc in range(8) for s in range(NSUB)]

    def load_uh(t0, s0, L):
        first = (t0 == s0)
        last = (t0 + T == s0 + L)
        lo = t0 - (0 if first else 8)
        hi = t0 + T + (0 if last else 8)
        c0 = 8 - (t0 - lo)
        R.dma("pool", UH[:, :, c0:c0 + (hi - lo)], us[:, lo:hi].rearrange("(kc p) t -> p kc t", p=128),
              reads=[("us", "all")], writes=[("UH",)])
        if first:
            R.op("dve", lambda e: e.memset(UH[:, :, 0:8], 0.0), writes=[("UHl",)])
        if last:
            R.op("dve", lambda e: e.memset(UH[:, :, T + 8:T + 16], 0.0), writes=[("UHr",)])

    def pool_mix(l, t0, s0, L):
        first = (t0 == s0)
        last = (t0 + T == s0 + L)
        NCOL = T + 16
        for g, w in enumerate(POOLW):
            half = w // 2
            ch = slice(2 * g, 2 * g + 2)
            src = UH
            bufs = [TA, TB]
            n = NCOL
            k = 1
            cur = None
            idx = 0
            while k < w:
                n = n - k
                dst = bufs[idx % 2]
                if cur is None:
                    R.op("dve", lambda e, dst=dst, n=n, k=k, ch=ch: e.tensor_tensor(
                        out=dst[:, :, 0:n], in0=UH[:, ch, 0:n], in1=UH[:, ch, k:k + n], op=ALU.add),
                        reads=[("UH",), ("UHl",), ("UHr",)], writes=[("TT", idx % 2)] + BKEYS)
                else:
                    R.op("dve", lambda e, dst=dst, cur=cur, n=n, k=k: e.tensor_tensor(
                        out=dst[:, :, 0:n], in0=cur[:, :, 0:n], in1=cur[:, :, k:k + n], op=ALU.add),
                        reads=[("TT", (idx - 1) % 2)], writes=[("TT", idx % 2)] + BKEYS)
                cur = dst
                idx += 1
                k *= 2
            sb = 8 - half
            tk = ("TT", (idx - 1) % 2)
            R.op("dve", lambda e, cur=cur, sb=sb, w=w, ch=ch: e.scalar_tensor_tensor(
                out=A[:, ch, :], in0=cur[:, :, sb:sb + T], scalar=1.0 / w, in1=UH[:, ch, 8:8 + T], op0=ALU.mult, op1=ALU.subtract),
                reads=[tk, ("UH",)], writes=[("A", kc, s) for kc in (2 * g, 2 * g + 1) for s in range(NSUB)])
            fix = []
            if first:
                fix += [(t, t + half) for t in range(half)]
            if last:
                fix += [(t, (T - t) + half) for t in range(T - half + 1, T)]
            for (t, cnt) in fix:
                R.op("dve", lambda e, cur=cur, sb=sb, t=t, cnt=cnt, ch=ch: e.scalar_tensor_tensor(
                    out=A[:, ch, t:t + 1], in0=cur[:, :, sb + t:sb + t + 1], scalar=1.0 / cnt, in1=UH[:, ch, 8 + t:9 + t],
                    op0=ALU.mult, op1=ALU.subtract),
                    reads=[tk, ("UH",)], writes=[("A", kc, t // 512) for kc in (2 * g, 2 * g + 1)])
        wg = wload(("wg", l), 2048)
        for g in range(4):
            for m2 in range(2):
                for s in range(NSUB):
                    b = nbank()
                    for k2 in range(2):
                        c = ((g * 2 + k2) * 2 + m2) * 128
                        R.op("pe", lambda e, c=c, wg=wg, g=g, k2=k2, s=s, b=b: e.matmul(
                            PS[:, b, :], W[:, wg, c:c + 128], A[:, 2 * g + k2, sl(s)], start=(k2 == 0), stop=(k2 == 1)),
                            reads=[("W", wg), ("A", 2 * g + k2, s)], writes=[("ps", b)])
                    R.op("act", lambda e, g=g, m2=m2, s=s, b=b: e.activation(out=Bz[:, 2 * g + m2, sl(s)], in_=PS[:, b, :], func=AF.Copy),
                         reads=[("ps", b)], writes=[("B", 2 * g + m2, s)] + ([("TT", 0), ("TT", 1)] if (g, m2, s) == (0, 0, 0) else []))

        def ev(m, s, b):
            R.op("dve", lambda e: e.scalar_tensor_tensor(
                out=X[:, m, sl(s)], in0=PS[:, b, :], scalar=SM[:, COL_PS + l * 8 + m:COL_PS + l * 8 + m + 1], in1=X[:, m, sl(s)],
                op0=ALU.mult, op1=ALU.add),
                reads=[("ps", b), ("X", m, s), ("SM",)], writes=[("X", m, s)])
        linear8(lambda m: ("wout", l, m), Bz, "B", ev)

    def attention():
        R.op("dve", lambda e: e.memset(QZ[:, :, :, :], 0.0), writes=[("QZ", 0), ("QZ", 1)])
        s0 = 0
        it = 0
        for L in seqs:
            nkt = L // 128
            for g in range(4):
                for hb in (0, 64):
                    R.dma("sp", KD[hb:hb + 64, g, 0:L], ks[g * 64:(g + 1) * 64, s0:s0 + L], reads=[("ks", "all")], writes=[("KD", g)])
            step = 16
            for k0 in range(0, nkt, step):
                k1 = min(nkt, k0 + step)
                R.dma("sp", VA[:, k0:k1, :, :].rearrange("p k g d -> p k (g d)"),
                      vs[s0 + k0 * 128:s0 + k1 * 128, :].rearrange("(k p) f -> p k f", p=128),
                      reads=[("vs", "all")], writes=[("VA",)])
            iters = [(g, hp, qt) for g in range(4) for hp in range(2) for qt in range(L // 512)]

            def load_q(k, s0=s0):
                g, hp, qt = iters[k]
                c = 2 * g + hp
                tq = s0 + qt * 512
                for r in range(2):
                    R.dma("sp", QZ[r * 64:(r + 1) * 64, (it0 + k) % 2, r, :], qs[c * 128 + r * 64:c * 128 + (r + 1) * 64, tq:tq + 512],
                          reads=[("qs", "all")], writes=[("QZ", (it0 + k) % 2)])

            it0 = it
            load_q(0)
            for k, (g, hp, qt) in enumerate(iters):
                c = 2 * g + hp
                tq = s0 + qt * 512
                qsl = it % 2
                it += 1
                if k + 1 < len(iters):
                    load_q(k + 1)
                LA = 2

                def QK(i, g=g, qsl=qsl):
                    j = i % 3
                    for r in range(2):
                        sb = 2 + 2 * j + r
                        R.op("pe", lambda e, r=r, sb=sb: e.matmul(PS[:, sb, :], KD[:, g, i * 128:(i + 1) * 128],
                                                                 QZ[:, qsl, r, :], start=True, stop=True),
                             reads=[("KD", g), ("QZ", qsl)], writes=[("ps", sb)])
                    R.op("act", lambda e: e.activation(out=PT[:, j, :], in_=PS[:, 2 + 2 * j:4 + 2 * j, :].rearrange("p a b -> p (a b)"), func=AF.Exp),
                         reads=[("ps", 2 + 2 * j), ("ps", 3 + 2 * j)], writes=[("PT", j)])

                def PV(i, g=g, nkt=nkt):
                    j = i % 3
                    for r in range(2):
                        R.op("pe", lambda e, r=r: e.matmul(PS[:, r, :], VA[:, i, g, :], PT[:, j, r * 512:(r + 1) * 512],
                                                           start=(i == 0), stop=(i == nkt - 1)),
                             reads=[("VA",), ("PT", j)], writes=[("ps", r)])

                for i in range(nkt + LA):
                    if i < nkt:
                        QK(i)
                    if i >= LA:
                        PV(i - LA)
                for r in range(2):
                    R.op("dve", lambda e, r=r, qsl=qsl: e.tensor_copy(out=OC[0:64, qsl, r, :], in_=PS[0:64, r, :]),
                         reads=[("ps", r)], writes=[("OC", qsl, r)])
                    R.op("dve", lambda e, r=r, qsl=qsl: e.tensor_copy(out=DC[0:64, qsl, r, :], in_=PS[64:128, r, :]),
                         reads=[("ps", r)], writes=[("DC", qsl, r)])
                for r in range(2):
                    R.op("dve", lambda e, r=r, qsl=qsl: e.reciprocal(out=REC[0:64, r, :], in_=DC[0:64, qsl, r, :]),
                         reads=[("DC", qsl, r)], writes=[("REC", r)])
                    R.op("dve", lambda e, r=r, qsl=qsl: e.tensor_tensor(
                        out=OST[r * 64:(r + 1) * 64, qsl, :], in0=OC[0:64, qsl, r, :], in1=REC[0:64, r, :], op=ALU.mult),
                        reads=[("OC", qsl, r), ("REC", r)], writes=[("OST", qsl)])
                R.dma("pool", os_[c * 128:(c + 1) * 128, tq:tq + 512], OST[:, qsl, :],
                      reads=[("OST", qsl)], writes=[("os", tq)])
            s0 += L

    def phase_ok(p):
        return phases is None or p in phases

    def vst_ones():
        R.op("dve", lambda e: e.memset(VSTF[:, :], 1.0), writes=[("VST", s, blk) for s in range(NSUB) for blk in range(4)])

    vst_ones()
    def nxt(k):
        return tiles[k + 1][0] if k + 1 < len(tiles) else None

    def load_cs(pos0):
        R.dma("pool", CS[:, :, :], rope[:, :, pos0:pos0 + T].rearrange("a p t -> p a t"), writes=[("CS",)])

    if phase_ok(1):
        load_x(xT, tiles[0][0])
        for k, (t0, s0, L) in enumerate(tiles):
            load_cs(t0 - s0)
            norm(0); ffn(0, 0)
            norm(1)
            store_x(xs, t0, "xs")
            if nxt(k) is not None:
                load_x(xT, nxt(k))
            qkv(0, t0, t0 - s0)
        R.barrier()
    if phase_ok(2):
        attention()
        R.barrier()
    if phase_ok(3):
        load_x(xs, tiles[0][0])
        load_o(tiles[0][0])
        for k, (t0, s0, L) in enumerate(tiles):
            wo(0, t0)
            if nxt(k) is not None:
                load_o(nxt(k))
            norm(2); ffn(0, 1)
            norm(3); ffn(1, 0)
            norm(4)
            store_x(xs, t0, "xs")
            if nxt(k) is not None:
                load_x(xs, nxt(k))
            pool_in(0, t0)
        R.barrier()
    if phase_ok(4):
        load_x(xs, tiles[0][0])
        load_uh(*tiles[0])
        for k, (t0, s0, L) in enumerate(tiles):
            load_cs(t0 - s0)
            pool_mix(0, t0, s0, L)
            if k + 1 < len(tiles):
                load_uh(*tiles[k + 1])
            norm(5); ffn(1, 1)
            norm(6); ffn(2, 0)
            norm(7)
            store_x(xs, t0, "xs")
            if nxt(k) is not None:
                load_x(xs, nxt(k))
            qkv(1, t0, t0 - s0)
        R.barrier()
    if phase_ok(5):
        attention()
        R.barrier()
    if phase_ok(6):
        load_x(xs, tiles[0][0])
        load_o(tiles[0][0])
        for k, (t0, s0, L) in enumerate(tiles):
            wo(1, t0)
            if nxt(k) is not None:
                load_o(nxt(k))
            norm(8); ffn(2, 1)
            norm(9); ffn(3, 0)
            norm(10)
            store_x(xs, t0, "xs")
            if nxt(k) is not None:
                load_x(xs, nxt(k))
            pool_in(1, t0)
        R.barrier()
    if phase_ok(7):
        load_uh(*tiles[0])
        for k, (t0, s0, L) in enumerate(tiles):
            load_x(xs, t0)
            pool_mix(1, t0, s0, L)
            if k + 1 < len(tiles):
                load_uh(*tiles[k + 1])
            norm(11); ffn(3, 1)
            store_x(yT, t0, "y")
        R.barrier()
    elif phases is not None:
        for (t0, s0, L) in tiles:
            load_x(xs, t0)
            store_x(yT, t0, "y")
        R.barrier()

    nsem = R.finalize()
    from contextlib import ExitStack
    with ExitStack() as es:
        sems = {}
        for e in Rec.ENGS:
            for ep in range(nsem[e]):
                sems[(e, ep)] = es.enter_context(nc.semaphore(f"s_{e}_{ep}"))
        for i in range(NDMASEM):
            sems[("dma", i)] = es.enter_context(nc.semaphore(f"s_dma_{i}"))
        block = es.enter_context(nc.Block())

        def replay(name, eng):
            waited = {}
            for o in R.streams[name]:
                for d in o.deps:
                    if waited.get(d.sem, 0) >= d.val:
                        continue
                    eng.wait_ge(sems[d.sem], d.val)
                    waited[d.sem] = d.val
                ins = o.fn(eng)
                if o.sig:
                    ins.then_inc(sems[o.sem], 16 if o.is_dma else 1)
            for d in R.pending[name]:
                if d.sem is None or waited.get(d.sem, 0) >= d.val:
                    continue
                eng.wait_ge(sems[d.sem], d.val)
                waited[d.sem] = d.val

        @block.sync
        def _(e):
            replay("sp", e)

        @block.scalar
        def _(e):
            replay("act", e)

        @block.vector
        def _(e):
            replay("dve", e)

        @block.gpsimd
        def _(e):
            replay("pool", e)

        @block.tensor
        def _(e):
            replay("pe", e)
    return nc


def make_in_maps(inputs, seqs, xts):
    wall = pack_weights(inputs)
    sm = pack_small(inputs)
    cm = const_mats()
    rp = rope_tables()
    return [{"xT": x, "wall": wall, "small": sm, "cmat": cm, "rope": rp} for x in xts]


def kernel(**inputs):
    inputs = {k: np.asarray(v) for k, v in inputs.items()}
    xp = inputs["x_prompt"]
    xsm = inputs["x_sample"]
    n = 8
    xts = []
    for c in range(n):
        rows = np.concatenate([xp[2 * c], xp[2 * c + 1], xsm[c]], axis=0)
        xts.append(np.ascontiguousarray(rows.T))
    nc = build_program(SEQS_FULL)
    in_maps = make_in_maps(inputs, SEQS_FULL, xts)
    res = run_bass_kernel_spmd(nc, in_maps, core_ids=list(range(n)))
    yp = np.empty_like(xp)
    ysm = np.empty_like(xsm)
    for c in range(n):
        y = np.asarray(res.results[c]["yT"]).T
        yp[2 * c] = y[0:4096]
        yp[2 * c + 1] = y[4096:8192]
        ysm[c] = y[8192:16384]
    return (yp, ysm)
```
